# Optimizing a Trainium2 kernel written in Bass

```python
import jax, jax.numpy as jnp
from jax import lax
import numpy as np

D_MODEL = 2048
BATCH = 2
SEQ = 16384
DEPTH = 1

PLE_DIM = 256
CHUNK = 128
SGU_GROUPS = 8
SGU_GROUP_DIM = 128
SGU_WIDTH = SGU_GROUPS * SGU_GROUP_DIM
SB_HEADS = 8
SB_HEAD_DIM = 128
SB_WIDTH = SB_HEADS * SB_HEAD_DIM
Q_BLOCK = 128
FFN_HIDDEN = ((8 * D_MODEL + 3 * 256 - 1) // (3 * 256)) * 256
EPS = 1e-6
IN_COLS = 2 * SGU_WIDTH + 3 * SB_WIDTH + 2 * D_MODEL

kernel_name = "hybrid_sgu_stickbreaking_block"


def rmsnorm(x, g):
    xf = x.astype(jnp.float32)
    y = xf * lax.rsqrt(jnp.mean(xf * xf, axis=-1, keepdims=True) + EPS)
    return (y * g.astype(jnp.float32)).astype(x.dtype)


def chunked_spatial_gating(u, v, w_s, b_s):
    B, S, G, C = v.shape
    tri = jnp.tril(jnp.ones((CHUNK, CHUNK), dtype=w_s.dtype))
    ws = w_s * tri[None]
    vc = v.reshape(B, S // CHUNK, CHUNK, G, C)
    mixed = jnp.einsum('gts,bnsgc->bntgc', ws, vc)
    mixed = mixed + jnp.transpose(b_s)[None, None, :, :, None]
    return u * mixed.reshape(B, S, G, C)


def stick_breaking_attention(q, k, v):
    B, S, H, Dh = q.shape
    nb = S // Q_BLOCK
    scale = Dh ** -0.5
    qf = jnp.transpose(q, (0, 2, 1, 3))
    kf = jnp.transpose(k, (0, 2, 1, 3))
    vf = jnp.transpose(v, (0, 2, 1, 3))
    key_pos = jnp.arange(S, dtype=jnp.int32)

    def one_block(blk):
        start = blk * Q_BLOCK
        qi = lax.dynamic_slice_in_dim(qf, start, Q_BLOCK, axis=2)
        z = jnp.einsum('bhqd,bhkd->bhqk', qi, kf).astype(jnp.float32) * scale
        q_pos = start + jnp.arange(Q_BLOCK, dtype=jnp.int32)
        strict = key_pos[None, :] < q_pos[:, None]
        log_beta = jax.nn.log_sigmoid(z)
        log_1m = jnp.where(strict, jax.nn.log_sigmoid(-z), jnp.zeros_like(z))
        tail = lax.cumsum(log_1m, axis=3, reverse=True) - log_1m
        w = jnp.where(strict, jnp.exp(log_beta + tail), jnp.zeros_like(z))
        return jnp.einsum('bhqk,bhkd->bhqd', w.astype(vf.dtype), vf)

    out = lax.map(one_block, jnp.arange(nb, dtype=jnp.int32))
    out = jnp.transpose(out, (1, 0, 3, 2, 4))
    return out.reshape(B, S, H, Dh)


def setup_inputs(seed: int = 0) -> dict:
    key = jax.random.key(seed)
    ks = jax.random.split(key, 20)
    f = jnp.float32

    def nrm(k, shape, fan_in):
        return jax.random.normal(k, shape, f) * (fan_in ** -0.5)

    def gain(k, shape):
        return 1.0 + 0.05 * jax.random.normal(k, shape, f)

    return {
        "x": jax.random.normal(ks[0], (BATCH, SEQ, D_MODEL), f),
        "p": jax.random.normal(ks[1], (DEPTH, BATCH, SEQ, PLE_DIM), f),
        "attn_norm_g": gain(ks[2], (DEPTH, D_MODEL)),
        "w_in": nrm(ks[3], (DEPTH, D_MODEL, IN_COLS), D_MODEL),
        "sgu_norm_g": gain(ks[4], (DEPTH, SGU_GROUPS, SGU_GROUP_DIM)),
        "w_s": nrm(ks[5], (DEPTH, SGU_GROUPS, CHUNK, CHUNK), CHUNK),
        "b_s": 1.0 + 0.1 * jax.random.normal(ks[6], (DEPTH, SGU_GROUPS, CHUNK), f),
        "q_norm_g": gain(ks[7], (DEPTH, SB_HEAD_DIM)),
        "k_norm_g": gain(ks[8], (DEPTH, SB_HEAD_DIM)),
        "w_up_a": nrm(ks[9], (DEPTH, SGU_WIDTH, D_MODEL), SGU_WIDTH),
        "w_up_b": nrm(ks[10], (DEPTH, SB_WIDTH, D_MODEL), SB_WIDTH),
        "w_o": nrm(ks[11], (DEPTH, D_MODEL, D_MODEL), D_MODEL),
        "ffn_norm_g": gain(ks[12], (DEPTH, D_MODEL)),
        "w_ffn_in": nrm(ks[13], (DEPTH, D_MODEL, 2 * FFN_HIDDEN), D_MODEL),
        "w_ffn_out": nrm(ks[14], (DEPTH, FFN_HIDDEN, D_MODEL), FFN_HIDDEN),
        "ple_norm_g": gain(ks[15], (DEPTH, D_MODEL)),
        "w_ple_gate": nrm(ks[16], (DEPTH, D_MODEL, D_MODEL), D_MODEL),
        "w_ple": nrm(ks[17], (DEPTH, PLE_DIM, D_MODEL), PLE_DIM),
    }


def reference(x, p, attn_norm_g, w_in, sgu_norm_g, w_s, b_s, q_norm_g, k_norm_g,
              w_up_a, w_up_b, w_o, ffn_norm_g, w_ffn_in, w_ffn_out,
              ple_norm_g, w_ple_gate, w_ple):
    B, S, D = x.shape
    c0 = SGU_WIDTH
    c1 = c0 + SGU_WIDTH
    c2 = c1 + SB_WIDTH
    c3 = c2 + SB_WIDTH
    c4 = c3 + SB_WIDTH
    c5 = c4 + D_MODEL
    for i in range(DEPTH):
        h = rmsnorm(x, attn_norm_g[i])
        proj = h @ w_in[i]
        u = proj[..., :c0]
        v_sgu = proj[..., c0:c1]
        q = proj[..., c1:c2]
        k = proj[..., c2:c3]
        v_att = proj[..., c3:c4]
        g_a = proj[..., c4:c5]
        g_b = proj[..., c5:]

        u = jax.nn.gelu(u).reshape(B, S, SGU_GROUPS, SGU_GROUP_DIM)
        v_sgu = rmsnorm(jax.nn.gelu(v_sgu).reshape(B, S, SGU_GROUPS, SGU_GROUP_DIM), sgu_norm_g[i])
        y_a = chunked_spatial_gating(u, v_sgu, w_s[i], b_s[i]).reshape(B, S, SGU_WIDTH)

        q = rmsnorm(q.reshape(B, S, SB_HEADS, SB_HEAD_DIM), q_norm_g[i])
        k = rmsnorm(k.reshape(B, S, SB_HEADS, SB_HEAD_DIM), k_norm_g[i])
        v_att = v_att.reshape(B, S, SB_HEADS, SB_HEAD_DIM)
        y_b = stick_breaking_attention(q, k, v_att).reshape(B, S, SB_WIDTH)

        merged = jax.nn.sigmoid(g_a) * (y_a @ w_up_a[i]) + jax.nn.sigmoid(g_b) * (y_b @ w_up_b[i])
        x = x + merged @ w_o[i]

        h = rmsnorm(x, ffn_norm_g[i])
        hid = h @ w_ffn_in[i]
        gate = hid[..., :FFN_HIDDEN]
        up = hid[..., FFN_HIDDEN:]
        x = x + (jax.nn.silu(gate) * up) @ w_ffn_out[i]

        ple = p[i] @ w_ple[i]
        x = x + jax.nn.sigmoid(rmsnorm(x, ple_norm_g[i]) @ w_ple_gate[i]) * ple
    return x
```

```python
from contextlib import ExitStack

import numpy as np
import concourse.bass as bass
import concourse.mybir as mybir
from concourse.bass_utils import run_bass_kernel_spmd

F32 = mybir.dt.float32
BF16 = mybir.dt.bfloat16
AF = mybir.ActivationFunctionType
ALU = mybir.AluOpType

D = 2048
DC = 16
NH = 8
FH = 5632
FC = 44
INC = 9216
TT = 512
EPS = 1e-6
NEG = -30000.0
import os
DEBUG = bool(int(os.environ.get('KDEBUG', '0')))
PHASES = os.environ.get('KPHASES', 'KATC')
CSTOP = int(os.environ.get('KCSTOP', '99'))


class Sem:
    def __init__(self, h, step=1):
        self.h = h
        self.step = step
        self.count = 0


class Buf:
    __slots__ = ("w", "r")

    def __init__(self):
        self.w = None
        self.r = []


def bufs(n):
    return [Buf() for _ in range(n)]


class Prog:
    ENG = ("pe", "act", "dve", "pool", "sp")

    def __init__(self, nc, esem):
        self.nc = nc
        self.esem = esem
        self.q = {k: [] for k in self.ENG}
        self.waited = {k: {} for k in self.ENG}
        self.dsems = []

    def group(self, eng, fns, reads=(), writes=(), dsem=None):
        deps = {}

        def add(d):
            if d is None:
                return
            s, v = d
            if deps.get(s, 0) < v:
                deps[s] = v

        for b in reads:
            add(b.w)
        for b in writes:
            add(b.w)
            for d in b.r:
                add(d)
        waited = self.waited[eng]
        wl = []
        for s, v in deps.items():
            if eng in ("pe", "sp") and s is self.esem.get(eng):
                continue
            if waited.get(s, 0) >= v:
                continue
            waited[s] = v
            wl.append((s.h, v))
        sem = dsem if dsem is not None else self.esem[eng]
        sem.count += sem.step
        val = sem.count
        h, step = sem.h, sem.step

        def thunk(e):
            for (sh, v) in wl:
                e.wait_ge(sh, v)
            ins = None
            for f in fns:
                ins = f(e)
            ins.then_inc(h, step)

        self.q[eng].append(thunk)
        tok = (sem, val)
        for b in writes:
            b.w = tok
            b.r = []
        for b in reads:
            b.r.append(tok)
        return tok

    def op(self, eng, fn, reads=(), writes=(), dsem=None):
        return self.group(eng, [fn], reads, writes, dsem)

    def wait_all(self, eng, toks):
        best = {}
        for t in toks:
            if t is None:
                continue
            s, v = t
            if best.get(s, 0) < v:
                best[s] = v
        wl = [(s.h, v) for s, v in best.items()]

        def thunk(e):
            for (sh, v) in wl:
                e.wait_ge(sh, v)

        self.q[eng].append(thunk)

    def run_block(self):
        nc = self.nc
        q = self.q
        with nc.Block() as blk:
            @blk.tensor
            def _(e):
                for f in q["pe"]:
                    f(e)

            @blk.scalar
            def _(e):
                for f in q["act"]:
                    f(e)

            @blk.vector
            def _(e):
                for f in q["dve"]:
                    f(e)

            @blk.gpsimd
            def _(e):
                for f in q["pool"]:
                    f(e)

            @blk.sync
            def _(e):
                for f in q["sp"]:
                    f(e)
        self.q = {k: [] for k in self.ENG}


def build_program(NQG):
    S = 2048 * NQG
    NOWN = 512 * NQG
    NKT = S // TT
    NKB = S // 128
    nc = bass.Bass("TRN2", target_bir_lowering=False)

    def din(name, shape):
        return nc.dram_tensor(name, list(shape), F32, kind="ExternalInput").ap()

    xb = din("xb", [S, D])
    xo = din("xo", [NOWN, D])
    po = din("po", [NOWN, 256])
    maskd = din("maskd", [128, 16, 512])
    g_attn = din("attn_norm_g", [D])
    w_in = din("w_in", [D, INC])
    sgu_g = din("sgu_norm_g", [8, 128])
    w_s = din("w_s", [8, 128, 128])
    b_s = din("b_s", [8, 128])
    qg = din("q_norm_g", [128])
    kg = din("k_norm_g", [128])
    w_up_a = din("w_up_a", [1024, D])
    w_up_b = din("w_up_b", [1024, D])
    w_o = din("w_o", [D, D])
    g_ffn = din("ffn_norm_g", [D])
    w_fi = din("w_ffn_in", [D, 2 * FH])
    w_fo = din("w_ffn_out", [FH, D])
    g_ple = din("ple_norm_g", [D])
    w_pg = din("w_ple_gate", [D, D])
    w_ple = din("w_ple", [256, D])
    out = nc.dram_tensor("out", [NOWN, D], F32, kind="ExternalOutput").ap()

    skind = "ExternalOutput" if DEBUG else "Internal"

    def scratch(name, shape, dt=BF16):
        return nc.dram_tensor(name, list(shape), dt, kind=skind).ap()

    kT_s = scratch("kT_s", [NH, 128, S])
    v_s = scratch("v_s", [S, 1024])
    qT_s = scratch("qT_s", [NH, 128, NOWN])
    ga_s = scratch("ga_s", [DC, 128, NOWN])
    sgb_s = scratch("sgb_s", [DC, 128, NOWN])
    yb_s = scratch("yb_s", [NH, 128, NOWN])

    def wcache(name, K, N):
        return nc.dram_tensor(name, [N // 512, 128, K // 128, 512], BF16, kind="Internal").ap()

    wc_in = wcache("wc_in", D, INC)
    wc_ua = wcache("wc_ua", 1024, D)
    wc_ub = wcache("wc_ub", 1024, D)
    wc_o = wcache("wc_o", D, D)
    wc_fi = wcache("wc_fi", D, 2 * FH)
    wc_fo = wcache("wc_fo", FH, D)
    wc_pg = wcache("wc_pg", D, D)
    wc_pl = wcache("wc_pl", 256, D)

    with ExitStack() as es:
        def sb(name, shape, dt):
            return es.enter_context(nc.sbuf_tensor(name, list(shape), dt))

        def newsem(name, step=1):
            return Sem(es.enter_context(nc.semaphore(name)), step)

        esem = {k: newsem("e_" + k) for k in ("pe", "act", "dve", "pool")}
        esem["sp"] = newsem("e_sp_unused")
        P = Prog(nc, esem)
        nds = [0]

        def dsem():
            nds[0] += 1
            return newsem("d%d" % nds[0], 16)

        ident_f = sb("ident_f", [128, 128], F32)
        ident_b = sb("ident_b", [128, 128], BF16)
        ones_b = sb("ones_b", [128, 128], BF16)
        negones = sb("negones", [128, 128], BF16)
        negtri = sb("negtri", [128, 128], BF16)
        gT = sb("gT", [128, 3, DC], F32)
        sgT = sb("sgT", [128, 8], F32)
        gq = sb("gq", [128, 2], F32)
        wsT = sb("wsT", [128, 8, 128], BF16)
        bsb = sb("bsb", [128, 1024], F32)
        wtmp = sb("wtmp", [128, 128], F32)
        wtmp2 = sb("wtmp2", [128, 128], F32)
        psum = es.enter_context(nc.psum_tensor("psum", [128, 8, 512], F32))
        psb = [Buf() for _ in range(8)]

        def ps(i, n=512):
            return psum[:, i, 0:n]

        def psbf(i, n):
            return psum[:, i, :].bitcast(BF16)[:, 0:n]

        B_const = Buf()
        gstage = sb("gstage", [16, 4, 128], F32)
        B_gst = bufs(4)
        d_gst = [dsem() for _ in range(4)]
        B_wtmp = Buf()
        B_wtmp2 = Buf()
        d_const = dsem()
        d_wtmp = dsem()

        ready = {}

        cast_jobs = []
        cast_sems = {}

        def cast_weight(key, wc, w, K, N, groups=None, after=()):
            KC = K // 128
            G = N // 512
            for g in (groups if groups is not None else range(G)):
                for k0 in range(0, KC, 16):
                    k1 = min(KC, k0 + 16)
                    src = w[k0 * 128:k1 * 128, g * 512:(g + 1) * 512].rearrange("(kc p) j -> p kc j", p=128)
                    dst = wc[g, :, k0:k1, :]
                    cast_jobs.append((key, src, dst))

        def emit_casts(n):
            for _ in range(min(n, len(cast_jobs))):
                key, src, dst = cast_jobs.pop(0)
                if key not in cast_sems:
                    cast_sems[key] = dsem()
                    ready[key] = Buf()
                tok = P.op("pool", lambda e, s=src, d=dst: e.dma_start(out=d, in_=s), dsem=cast_sems[key])
                ready[key].w = tok

        def setup_consts():
            P.op("pool", lambda e: e.memset(ident_f[:], 1.0), writes=[B_const])
            P.op("pool", lambda e: e.affine_select(out=ident_f[:], in_=ident_f[:], pattern=[[1, 128]],
                                                   compare_op=ALU.is_equal, fill=0.0, base=0,
                                                   channel_multiplier=-1), writes=[B_const])
            P.op("pool", lambda e: e.tensor_copy(out=ident_b[:], in_=ident_f[:]), reads=[B_const], writes=[B_const])
            P.op("pool", lambda e: e.memset(ones_b[:], 1.0), writes=[B_const])
            P.op("pool", lambda e: e.memset(negones[:], -1.0), writes=[B_const])
            P.op("pool", lambda e: e.memset(negtri[:], -1.0), writes=[B_const])
            P.op("pool", lambda e: e.affine_select(out=negtri[:], in_=negtri[:], pattern=[[-1, 128]],
                                                   compare_op=ALU.is_ge, fill=0.0, base=0,
                                                   channel_multiplier=1), writes=[B_const])
            for i, g in enumerate((g_attn, g_ffn, g_ple, sgu_g)):
                rows = 8 if i == 3 else DC
                src = g if i == 3 else g.rearrange("(c p) -> c p", p=128)
                P.op("sp", lambda e, i=i, src=src, rows=rows: e.dma_start(out=gstage[0:rows, i, :], in_=src),
                     writes=[B_gst[i]], dsem=d_gst[i])
                P.op("pe", lambda e, i=i, rows=rows: e.transpose(ps(7, rows), gstage[0:rows, i, :],
                                                                 ident_f[0:rows, 0:rows]),
                     reads=[B_gst[i], B_const], writes=[psb[7]])
                dst = sgT[:] if i == 3 else gT[:, i, :]
                P.op("dve", lambda e, dst=dst, rows=rows: e.tensor_copy(out=dst, in_=ps(7, rows)),
                     reads=[psb[7]], writes=[B_const])
            P.op("sp", lambda e: e.dma_start(out=gq[:, 0:1], in_=qg.rearrange("(p o) -> p o", o=1)),
                 writes=[B_const], dsem=d_const)
            P.op("sp", lambda e: e.dma_start(out=gq[:, 1:2], in_=kg.rearrange("(p o) -> p o", o=1)),
                 writes=[B_const], dsem=d_const)
            P.op("sp", lambda e: e.dma_start(out=bsb[:], in_=b_s.rearrange("g t -> (g t)").partition_broadcast(128)),
                 writes=[B_const], dsem=d_const)
            P.op("dve", lambda e: e.scalar_tensor_tensor(out=gq[:, 0:1], in0=gq[:, 0:1], scalar=float(128 ** -0.5),
                                                         in1=gq[:, 1:2], op0=ALU.mult, op1=ALU.mult),
                 reads=[B_const], writes=[B_const])
            for g in range(8):
                P.op("sp", lambda e, g=g: e.dma_start(out=wtmp[:], in_=w_s[g]), writes=[B_wtmp], dsem=d_wtmp)
                P.op("pe", lambda e: e.transpose(ps(7, 128), wtmp[:], ident_f[:]), reads=[B_wtmp, B_const],
                     writes=[psb[7]])
                P.op("dve", lambda e: e.tensor_copy(out=wtmp2[:], in_=ps(7, 128)), reads=[psb[7]], writes=[B_wtmp2])
                P.op("pool", lambda e, g=g: e.affine_select(out=wsT[:, g, :], in_=wtmp2[:], pattern=[[1, 128]],
                                                            compare_op=ALU.is_ge, fill=0.0, base=0,
                                                            channel_multiplier=-1),
                     reads=[B_wtmp2], writes=[B_const])

        def rstd_from_ps(bank, dst, dstB, nfeat):
            P.op("act", lambda e: e.activation(out=dst, in_=ps(bank), func=AF.Ln, bias=EPS, scale=1.0 / nfeat),
                 reads=[psb[bank]], writes=[dstB])
            P.op("act", lambda e: e.activation(out=dst, in_=dst, func=AF.Exp, scale=-0.5),
                 reads=[dstB], writes=[dstB])

        class Ring:
            def __init__(self, alloc, name, nslots):
                self.t = alloc(name, [128, nslots, 16, 512], BF16)
                self.B = bufs(nslots)
                self.sem = [dsem() for _ in range(nslots)]
                self.n = nslots
                self.i = 0

            def load(self, key, wc, g, k0, k1):
                s = self.i % self.n
                self.i += 1
                cb = ready[key]
                P.op("sp", lambda e, s=s: e.dma_start(out=self.t[:, s, 0:k1 - k0, :], in_=wc[g, :, k0:k1, :]),
                     reads=[cb], writes=[self.B[s]], dsem=self.sem[s])
                return s

        def norm_tm(src_rows, gi, xs, xsB, xsS, xn, xnB, junk, junkB, ss, ssB, hT, hTB, trbank, part="both"):
            if part in ("both", "pre"):
                norm_tm_pre(src_rows, xs, xsB, xsS, xn, xnB, junk, junkB, ss, ssB)
            if part in ("both", "post"):
                norm_tm_post(gi, xn, xnB, hT, hTB, trbank)

        def norm_tm_pre(src_rows, xs, xsB, xsS, xn, xnB, junk, junkB, ss, ssB):
            P.op("dve", lambda e: e.memset(ss[:], 0.0), writes=[ssB])
            for sub in range(4):
                k = sub % 2
                P.op("sp", lambda e, sub=sub, k=k: e.dma_start(out=xs[:, k, :], in_=src_rows[sub * 128:(sub + 1) * 128, :]),
                     writes=[xsB[k]], dsem=xsS[k])
                P.op("act", lambda e, sub=sub, k=k: e.activation(out=junk[:], in_=xs[:, k, :], func=AF.Square,
                                                                 accum_out=ss[:, sub:sub + 1]),
                     reads=[xsB[k]], writes=[junkB, ssB])
                P.op("act", lambda e, sub=sub: e.activation(out=ss[:, 4 + sub:5 + sub], in_=ss[:, sub:sub + 1],
                                                            func=AF.Ln, bias=EPS, scale=1.0 / D),
                     reads=[ssB], writes=[ssB])
                P.op("act", lambda e, sub=sub: e.activation(out=ss[:, 4 + sub:5 + sub], in_=ss[:, 4 + sub:5 + sub],
                                                            func=AF.Exp, scale=-0.5),
                     reads=[ssB], writes=[ssB])
                P.op("dve", lambda e, sub=sub, k=k: e.tensor_scalar(out=xn[:, sub, :], in0=xs[:, k, :],
                                                                    scalar1=ss[:, 4 + sub:5 + sub], scalar2=None,
                                                                    op0=ALU.mult),
                     reads=[xsB[k], ssB], writes=[xnB[sub]])

        def norm_tm_post(gi, xn, xnB, hT, hTB, trbank):
            for c in range(DC):
                bank = trbank[c % len(trbank)]
                P.group("pe", [lambda e, c=c, sub=sub, bank=bank: e.transpose(
                    psbf(bank, 512)[:, sub * 128:(sub + 1) * 128], xn[:, sub, c * 128:(c + 1) * 128], ident_b[:])
                    for sub in range(4)], reads=xnB + [B_const], writes=[psb[bank]])
                if c % 2 == 0:
                    P.op("dve", lambda e, c=c, bank=bank: e.tensor_scalar(out=hT[:, c, :], in0=psbf(bank, 512),
                                                                          scalar1=gT[:, gi, c:c + 1], scalar2=None,
                                                                          op0=ALU.mult),
                         reads=[psb[bank], B_const], writes=[hTB[c]])
                else:
                    P.op("act", lambda e, c=c, bank=bank: e.activation(out=hT[:, c, :], in_=psbf(bank, 512),
                                                                       func=AF.Copy, scale=gT[:, gi, c:c + 1]),
                         reads=[psb[bank], B_const], writes=[hTB[c]])

        with ExitStack() as es2:
          if 'K' in PHASES:
            def sb2(name, shape, dt):
                return es2.enter_context(nc.sbuf_tensor("ph0_" + name, list(shape), dt))

            wkv = sb2("wkv", [128, 4, 16, 512], BF16)
            wkvB = bufs(4)
            xs = sb2("xs", [128, 2, D], F32)
            xsB = bufs(2)
            xsS = [dsem(), dsem()]
            xn = sb2("xn", [128, 4, D], BF16)
            xnB = bufs(4)
            junk = sb2("junk", [128, D], BF16)
            junkB = Buf()
            ss = sb2("ss", [128, 8], F32)
            ssB = Buf()
            hT2 = sb2("hT", [128, 2, DC, 512], BF16)
            hTB2 = [bufs(DC), bufs(DC)]
            sqk = sb2("sqk", [128, 2, 512], BF16)
            sqkB = bufs(2)
            rsb = sb2("rsb", [128, 2, 512], F32)
            rsbB = bufs(2)
            kst = sb2("kst", [128, 2, 512], BF16)
            kstB = bufs(2)
            kstS = [dsem(), dsem()]
            vst = sb2("vst", [128, 2, 1024], BF16)
            vstB = bufs(2)
            vstS = [dsem(), dsem()]
            d_wkv = [dsem() for _ in range(4)]

            wstage = sb2("wstage", [128, 16, 512], F32)
            wstageB = Buf()
            d_wst = dsem()
            for i, g in enumerate((6, 7, 8, 9)):
                P.op("sp", lambda e, g=g: e.dma_start(
                    out=wstage[:], in_=w_in[:, g * 512:(g + 1) * 512].rearrange("(kc p) j -> p kc j", p=128)),
                    writes=[wstageB], dsem=d_wst)
                P.op("dve", lambda e, i=i: e.tensor_copy(out=wkv[:, i, 0:8, :], in_=wstage[:, 0:8, :]),
                     reads=[wstageB], writes=[wkvB[i]])
                P.op("act", lambda e, i=i: e.activation(out=wkv[:, i, 8:16, :], in_=wstage[:, 8:16, :], func=AF.Copy),
                     reads=[wstageB, wkvB[i]], writes=[wkvB[i]])
            setup_consts()
            cast_weight("in", wc_in, w_in, D, INC, groups=[0, 1, 2, 3, 4, 5] + list(range(10, 18)), after=wkvB)
            cast_weight("ua", wc_ua, w_up_a, 1024, D)
            cast_weight("ub", wc_ub, w_up_b, 1024, D)
            cast_weight("o", wc_o, w_o, D, D)
            cast_weight("fi", wc_fi, w_fi, D, 2 * FH)
            cast_weight("fo", wc_fo, w_fo, FH, D)
            cast_weight("pg", wc_pg, w_pg, D, D)
            cast_weight("pl", wc_pl, w_ple, 256, D)
            ukv = [0, 0]

            def kv_norm(t, part):
                norm_tm(xb[t * 512:(t + 1) * 512, :], 0, xs, xsB, xsS, xn, xnB, junk, junkB, ss, ssB,
                        hT2[:, t % 2], hTB2[t % 2], [6, 7], part=part)

            def kv_tile(t):
                uk, uv = ukv
                hT = hT2[:, t % 2]
                hTB = hTB2[t % 2]
                if t == 0:
                    kv_norm(0, "both")
                def k_main(hd, k):
                    bank = (uk + hd) % 3
                    wi, wo = divmod(hd * 128, 512)
                    P.group("pe", [lambda e, c=c: e.matmul(
                        ps(bank), lhsT=wkv[:, wi, c, wo:wo + 128], rhs=hT[:, c, :], start=(c == 0), stop=(c == DC - 1))
                        for c in range(DC)], reads=hTB + [wkvB[wi]], writes=[psb[bank]])
                    P.op("act", lambda e: e.activation(out=sqk[:, k, :], in_=ps(bank), func=AF.Square),
                         reads=[psb[bank]], writes=[sqkB[k]])

                def k_tail(hd, k):
                    bank = (uk + hd) % 3
                    sbank = 3
                    P.op("pe", lambda e: e.matmul(ps(sbank), lhsT=ones_b[:], rhs=sqk[:, k, :], start=True, stop=True),
                         reads=[sqkB[k], B_const], writes=[psb[sbank]])
                    rstd_from_ps(sbank, rsb[:, k, :], rsbB[k], 128)
                    P.op("dve", lambda e: e.tensor_tensor(out=kst[:, k, :], in0=ps(bank), in1=rsb[:, k, :], op=ALU.mult),
                         reads=[psb[bank], rsbB[k]], writes=[kstB[k]])
                    P.op("pool", lambda e: e.dma_start(out=kT_s[hd, :, t * 512:(t + 1) * 512], in_=kst[:, k, :]),
                         reads=[kstB[k]], writes=[], dsem=kstS[k])

                for hd in range(NH + 1):
                    if hd == 2 and t + 1 < NKT:
                        kv_norm(t + 1, "pre")
                    if hd < NH:
                        k_main(hd, (uk + hd) % 2)
                    if hd >= 1:
                        k_tail(hd - 1, (uk + hd - 1) % 2)
                uk += NH
                if t + 1 < NKT:
                    kv_norm(t + 1, "post")
                for sub in range(4):
                    k = uv % 2
                    uv += 1
                    for cg in range(2):
                        bank = 4 + cg
                        P.group("pe", [lambda e, c=c, sub=sub, cg=cg, bank=bank: e.matmul(
                            ps(bank), lhsT=hT[:, c, sub * 128:(sub + 1) * 128], rhs=wkv[:, 2 + cg, c, :],
                            start=(c == 0), stop=(c == DC - 1)) for c in range(DC)],
                            reads=hTB + [wkvB[2 + cg]], writes=[psb[bank]])
                        if cg == 0:
                            P.op("dve", lambda e, k=k, bank=bank: e.tensor_copy(out=vst[:, k, 0:512], in_=ps(bank)),
                                 reads=[psb[bank]], writes=[vstB[k]])
                        else:
                            P.op("act", lambda e, k=k, bank=bank: e.activation(out=vst[:, k, 512:1024], in_=ps(bank),
                                                                               func=AF.Copy),
                                 reads=[psb[bank], vstB[k]], writes=[vstB[k]])
                    r0 = t * 512 + sub * 128
                    P.op("pool", lambda e, k=k, r0=r0: e.dma_start(out=v_s[r0:r0 + 128, :], in_=vst[:, k, :]),
                         reads=[vstB[k]], writes=[], dsem=vstS[k])
                ukv[0], ukv[1] = uk, uv

            per_tile = (len(cast_jobs) + NKT - 2) // max(1, NKT - 1)
            for t in range(NKT):
                kv_tile(t)
                emit_casts(per_tile)
            emit_casts(len(cast_jobs))
            P.wait_all("pool", [(s, s.count) for s in kstS + vstS])
            P.run_block()

        with ExitStack() as es2:
          if 'A' in PHASES:
            def sb2(name, shape, dt):
                return es2.enter_context(nc.sbuf_tensor("ph1_" + name, list(shape), dt))

            ring = None
            xs = sb2("xs", [128, 2, D], F32)
            xsB = bufs(2)
            xsS = [dsem(), dsem()]
            xn = sb2("xn", [128, 4, D], BF16)
            xnB = bufs(4)
            junk = sb2("junk", [128, D], BF16)
            junkB = Buf()
            ss = sb2("ss", [128, 8], F32)
            ssB = Buf()
            hT2 = sb2("hT", [128, 2, DC, 512], BF16)
            hTB2 = [bufs(DC), bufs(DC)]
            u_sb = sb2("u_sb", [128, 8, 512], BF16)
            uB = bufs(8)
            ya = sb2("ya", [128, 8, 512], BF16)
            yaB = bufs(8)
            sga = sb2("sga", [128, DC, 512], BF16)
            sgaB = bufs(DC)
            gv = sb2("gv", [128, 4, 512], F32)
            gvB = bufs(4)
            sq = sb2("sq", [128, 4, 512], BF16)
            sqB = bufs(4)
            rsb = sb2("rsb", [128, 4, 512], F32)
            rsbB = bufs(4)
            vn = sb2("vn", [128, 2, 512], BF16)
            vnB = bufs(2)
            vtok = sb2("vtok", [128, 2, 512], BF16)
            vtokB = bufs(2)
            tmpf = sb2("tmpf", [128, 2, 512], F32)
            tmpfB = bufs(2)
            stg = sb2("stg", [128, 4, 512], BF16)
            stgB = bufs(4)
            stgS = [dsem() for _ in range(4)]
            ring = Ring(sb2, "ringA", 3)
            stg_i = [0]

            def store_stage(dst_ap_fn, producer):
                k = stg_i[0] % 4
                stg_i[0] += 1
                producer(k)
                P.op("pool", lambda e, k=k: e.dma_start(out=dst_ap_fn(), in_=stg[:, k, :]),
                     reads=[stgB[k]], writes=[], dsem=stgS[k])

            obank = [0]

            pending = []

            def run_pending():
                for st in list(pending):
                    st.pop(0)()
                    if not st:
                        pending.remove(st)

            def flush_pending():
                while pending:
                    run_pending()

            def layer(key, wc, groups, KC, in_ap, inB, evac):
                ci = 0
                for g in groups:
                    s = ring.load(key, wc, g, 0, KC)
                    for j in range(4):
                        bank = obank[0] % 4
                        obank[0] += 1
                        P.group("pe", [lambda e, c=c, s=s, j=j, bank=bank: e.matmul(
                            ps(bank), lhsT=ring.t[:, s, c, j * 128:(j + 1) * 128], rhs=in_ap(c),
                            start=(c == 0), stop=(c == KC - 1)) for c in range(KC)],
                            reads=list(inB) + [ring.B[s]], writes=[psb[bank]])
                        run_pending()
                        st = evac(ci, bank)
                        if st:
                            pending.append(list(st))
                        ci += 1

            def a_norm(t, part):
                norm_tm(xo[t * 512:(t + 1) * 512, :], 0, xs, xsB, xsS, xn, xnB, junk, junkB, ss, ssB,
                        hT2[:, t % 2], hTB2[t % 2], [6, 7], part=part)

            def a_tile(t):
                c0 = t * 512
                hT = hT2[:, t % 2]
                hTB = hTB2[t % 2]
                if t == 0:
                    a_norm(0, "both")

                def ev_u(ci, bank):
                    P.op("act", lambda e: e.activation(out=u_sb[:, ci, :], in_=ps(bank), func=AF.Gelu),
                         reads=[psb[bank]], writes=[uB[ci]])
                layer("in", wc_in, [0, 1], DC, lambda c: hT[:, c, :], hTB, ev_u)

                def ev_v(ci, bank):
                    k = ci % 4
                    k2 = ci % 2
                    P.op("act", lambda e: e.activation(out=gv[:, k, :], in_=ps(bank), func=AF.Gelu),
                         reads=[psb[bank]], writes=[gvB[k]])
                    P.op("act", lambda e: e.activation(out=sq[:, k, :], in_=gv[:, k, :], func=AF.Square),
                         reads=[gvB[k]], writes=[sqB[k]])

                    def s1():
                        P.op("pe", lambda e: e.matmul(ps(4), lhsT=ones_b[:], rhs=sq[:, k, :], start=True, stop=True),
                             reads=[sqB[k], B_const], writes=[psb[4]])
                        rstd_from_ps(4, rsb[:, k, :], rsbB[k], 128)
                        P.op("dve", lambda e: e.scalar_tensor_tensor(out=vn[:, k2, :], in0=gv[:, k, :],
                                                                     scalar=sgT[:, ci:ci + 1], in1=rsb[:, k, :],
                                                                     op0=ALU.mult, op1=ALU.mult),
                             reads=[gvB[k], rsbB[k], B_const], writes=[vnB[k2]])

                    def s2():
                        P.group("pe", [lambda e, sub=sub: e.transpose(psbf(5, 512)[:, sub * 128:(sub + 1) * 128],
                                                                      vn[:, k2, sub * 128:(sub + 1) * 128], ident_b[:])
                                       for sub in range(4)], reads=[vnB[k2], B_const], writes=[psb[5]])
                        P.op("dve", lambda e: e.tensor_copy(out=vtok[:, k2, :], in_=psbf(5, 512)),
                             reads=[psb[5]], writes=[vtokB[k2]])

                    def s3():
                        P.group("pe", [lambda e, sub=sub: e.matmul(ps(6)[:, sub * 128:(sub + 1) * 128],
                                                                   lhsT=vtok[:, k2, sub * 128:(sub + 1) * 128],
                                                                   rhs=wsT[:, ci, :], start=True, stop=True)
                                       for sub in range(4)], reads=[vtokB[k2], B_const], writes=[psb[6]])
                        P.group("dve", [lambda e, sub=sub: e.tensor_tensor(
                            out=tmpf[:, k2, sub * 128:(sub + 1) * 128], in0=ps(6)[:, sub * 128:(sub + 1) * 128],
                            in1=bsb[:, ci * 128:(ci + 1) * 128], op=ALU.add) for sub in range(4)],
                            reads=[psb[6], B_const], writes=[tmpfB[k2]])
                        P.op("dve", lambda e: e.tensor_tensor(out=ya[:, ci, :], in0=tmpf[:, k2, :], in1=u_sb[:, ci, :],
                                                              op=ALU.mult),
                             reads=[tmpfB[k2], uB[ci]], writes=[yaB[ci]])
                    return [s1, s2, s3]
                layer("in", wc_in, [2, 3], DC, lambda c: hT[:, c, :], hTB, ev_v)

                def ev_q(ci, bank):
                    k = ci % 4
                    P.op("act", lambda e: e.activation(out=sq[:, k, :], in_=ps(bank), func=AF.Square),
                         reads=[psb[bank]], writes=[sqB[k]])

                    def s1():
                        P.op("pe", lambda e: e.matmul(ps(4), lhsT=ones_b[:], rhs=sq[:, k, :], start=True, stop=True),
                             reads=[sqB[k], B_const], writes=[psb[4]])
                        rstd_from_ps(4, rsb[:, k, :], rsbB[k], 128)
                        store_stage(lambda: qT_s[ci, :, c0:c0 + 512],
                                    lambda kk: P.op("dve", lambda e: e.scalar_tensor_tensor(
                                        out=stg[:, kk, :], in0=ps(bank), scalar=gq[:, 0:1], in1=rsb[:, k, :],
                                        op0=ALU.mult, op1=ALU.mult), reads=[psb[bank], rsbB[k], B_const],
                                        writes=[stgB[kk]]))
                    return [s1]
                layer("in", wc_in, [4, 5], DC, lambda c: hT[:, c, :], hTB, ev_q)

                if t + 1 < NQG:
                    a_norm(t + 1, "pre")
                def ev_ga(ci, bank):
                    P.op("act", lambda e: e.activation(out=sga[:, ci, :], in_=ps(bank), func=AF.Sigmoid),
                         reads=[psb[bank]], writes=[sgaB[ci]])
                layer("in", wc_in, [10, 11, 12, 13], DC, lambda c: hT[:, c, :], hTB, ev_ga)

                if t + 1 < NQG:
                    a_norm(t + 1, "post")
                def ev_gb(ci, bank):
                    store_stage(lambda: sgb_s[ci, :, c0:c0 + 512],
                                lambda kk: P.op("act", lambda e: e.activation(out=stg[:, kk, :], in_=ps(bank),
                                                                               func=AF.Sigmoid),
                                                reads=[psb[bank]], writes=[stgB[kk]]))
                layer("in", wc_in, [14, 15, 16, 17], DC, lambda c: hT[:, c, :], hTB, ev_gb)

                def ev_ua(ci, bank):
                    store_stage(lambda: ga_s[ci, :, c0:c0 + 512],
                                lambda kk: P.op("dve", lambda e: e.tensor_tensor(out=stg[:, kk, :], in0=ps(bank),
                                                                                 in1=sga[:, ci, :], op=ALU.mult),
                                                reads=[psb[bank], sgaB[ci]], writes=[stgB[kk]]))
                flush_pending()
                layer("ua", wc_ua, [0, 1, 2, 3], 8, lambda c: ya[:, c, :], yaB, ev_ua)
                flush_pending()

            for t in range(NQG):
                a_tile(t)
            P.wait_all("pool", [(s, s.count) for s in stgS])
            P.run_block()

        with ExitStack() as es2:
          if 'T' in PHASES:
            def sb2(name, shape, dt):
                return es2.enter_context(nc.sbuf_tensor("ph2_" + name, list(shape), dt))

            masks = sb2("masks", [128, 16, 512], BF16)
            maskB = Buf()
            d_mask = dsem()
            KT = sb2("KT", [128, 2, S], BF16)
            KTB = bufs(2)
            KTS = [dsem(), dsem()]
            VV = sb2("VV", [128, 2, NKB, 128], BF16)
            VVB = bufs(2)
            VVS = [dsem(), dsem()]
            qT = sb2("qT", [128, 2, 512], BF16)
            qTB = bufs(2)
            qTS = [dsem(), dsem()]
            Eb = sb2("Eb", [128, 3, 2, 512], F32)
            EB = bufs(3)
            XC = sb2("XC", [128, 2, 2, 512], F32)
            XCB = bufs(2)
            SPs = sb2("SPs", [128, 2, 512], BF16)
            SPsB = bufs(2)
            SPt = sb2("SPt", [128, 2, 512], BF16)
            SPtB = bufs(2)
            SPb = sb2("SPb", [128, 2, 2, 512], BF16)
            SPB = bufs(2)
            Wb = sb2("Wb", [128, 2, 2, 512], BF16)
            WB = bufs(2)
            yst = sb2("yst", [128, 2, 512], BF16)
            ystB = bufs(2)
            ystS = [dsem(), dsem()]

            P.op("pool", lambda e: e.dma_start(out=masks[:], in_=maskd), writes=[maskB], dsem=d_mask)

            def load_head(hd):
                k = hd % 2
                nsplit = max(1, S // 4096)
                for i in range(nsplit):
                    a, b = i * (S // nsplit), (i + 1) * (S // nsplit)
                    P.op("sp", lambda e, a=a, b=b, k=k: e.dma_start(out=KT[:, k, a:b], in_=kT_s[hd, :, a:b]),
                         writes=[KTB[k]], dsem=KTS[k])
                vsrc = v_s.rearrange("(b p) c -> p b c", p=128)
                for b0 in range(0, NKB, 16):
                    P.op("sp", lambda e, b0=b0, k=k: e.dma_start(out=VV[:, k, b0:b0 + 16, :],
                                                                 in_=vsrc[:, b0:b0 + 16, hd * 128:(hd + 1) * 128]),
                         writes=[VVB[k]], dsem=VVS[k])

            load_head(0)
            def do_head(hd, gq_i):
                hk = hd % 2
                if hd + 1 < NH:
                    load_head(hd + 1)
                units = []
                for m in range(NQG):
                    nk = 16 * m + 16
                    for idx in range(nk // 2):
                        ka = nk - 1 - 2 * idx
                        kb_ = ka - 1
                        units.append(dict(m=m, ka=ka, kb=kb_, first=(idx == 0), last=(idx == nk // 2 - 1),
                                          ra=(ka - 16 * m) if ka >= 16 * m else None,
                                          rb=(kb_ - 16 * m) if kb_ >= 16 * m else None, qi=gq_i + m))
                U = len(units)

                def load_q(m):
                    qk = (gq_i + m) % 2
                    P.op("sp", lambda e: e.dma_start(out=qT[:, qk, :], in_=qT_s[hd, :, m * 512:(m + 1) * 512]),
                         writes=[qTB[qk]], dsem=qTS[qk])
                load_q(0)

                def qk1(u):
                    un = units[u]
                    qk = un["qi"] % 2
                    zb = 2 * (u % 2)
                    fns = []
                    rd = [KTB[hk], qTB[qk]]
                    for half, (kk, rr) in enumerate(((un["ka"], un["ra"]), (un["kb"], un["rb"]))):
                        fns.append(lambda e, kk=kk, rr=rr, half=half: e.matmul(
                            ps(zb + half), lhsT=KT[:, hk, kk * 128:(kk + 1) * 128], rhs=qT[:, qk, :],
                            start=True, stop=(rr is None)))
                        if rr is not None:
                            fns.append(lambda e, rr=rr, half=half: e.matmul(
                                ps(zb + half), lhsT=ident_b[:], rhs=masks[:, rr, :], start=False, stop=True))
                            rd += [maskB, B_const]
                    P.group("pe", fns, reads=rd, writes=[psb[zb], psb[zb + 1]])

                def act_e(u):
                    zb = 2 * (u % 2)
                    ek = u % 3
                    P.op("act", lambda e: e.activation(out=Eb[:, ek, :, :], in_=psum[:, zb:zb + 2, :], func=AF.Exp),
                         reads=[psb[zb], psb[zb + 1]], writes=[EB[ek]])

                def act_sp(u):
                    k = u % 2
                    ek = u % 3
                    un = units[u]
                    P.op("act", lambda e: e.activation(out=SPb[:, k, :, :], in_=Eb[:, ek, :, :], func=AF.Ln, bias=1.0,
                                                       scale=1.0),
                         reads=[EB[ek]], writes=[SPB[k]])
                    if not un["last"]:
                        if un["first"]:
                            P.op("dve", lambda e: e.tensor_tensor(out=SPs[:, k, :], in0=SPb[:, k, 0, :],
                                                                  in1=SPb[:, k, 1, :], op=ALU.add),
                                 reads=[SPB[k]], writes=[SPsB[k]])
                        else:
                            P.op("dve", lambda e: e.tensor_tensor(out=SPt[:, k, :], in0=SPb[:, k, 0, :],
                                                                  in1=SPb[:, k, 1, :], op=ALU.add),
                                 reads=[SPB[k]], writes=[SPtB[k]])
                            P.op("pool", lambda e: e.tensor_tensor(out=SPs[:, k, :], in0=SPs[:, 1 - k, :],
                                                                   in1=SPt[:, k, :], op=ALU.add),
                                 reads=[SPtB[k], SPsB[1 - k]], writes=[SPsB[k]])

                def cgroup(u):
                    un = units[u]
                    k = u % 2
                    first = un["first"]
                    fns = [lambda e: e.matmul(ps(4), lhsT=negtri[:], rhs=SPb[:, k, 0, :], start=True, stop=first)]
                    if not first:
                        fns.append(lambda e: e.matmul(ps(4), lhsT=negones[:], rhs=SPs[:, 1 - k, :], start=False,
                                                      stop=True))
                    fns.append(lambda e: e.matmul(ps(5), lhsT=negtri[:], rhs=SPb[:, k, 1, :], start=True, stop=False))
                    fns.append(lambda e: e.matmul(ps(5), lhsT=negones[:], rhs=SPb[:, k, 0, :], start=False,
                                                  stop=first))
                    rd = [SPB[k], B_const]
                    if not first:
                        fns.append(lambda e: e.matmul(ps(5), lhsT=negones[:], rhs=SPs[:, 1 - k, :], start=False,
                                                      stop=True))
                        rd.append(SPsB[1 - k])
                    P.group("pe", fns, reads=rd, writes=[psb[4], psb[5]])

                def act_w(u):
                    k = u % 2
                    ek = u % 3
                    P.op("act", lambda e: e.activation(out=XC[:, k, :, :], in_=psum[:, 4:6, :], func=AF.Exp),
                         reads=[psb[4], psb[5]], writes=[XCB[k]])
                    P.op("dve", lambda e: e.tensor_tensor(out=Wb[:, k, :, :], in0=Eb[:, ek, :, :], in1=XC[:, k, :, :],
                                                          op=ALU.mult),
                         reads=[EB[ek], XCB[k]], writes=[WB[k]])

                def wv(u):
                    un = units[u]
                    k = u % 2
                    obank = 6 + un["qi"] % 2
                    P.group("pe", [
                        lambda e: e.matmul(ps(obank), lhsT=VV[:, hk, un["ka"], :], rhs=Wb[:, k, 0, :],
                                           start=un["first"], stop=False, skip_group_check=True),
                        lambda e: e.matmul(ps(obank), lhsT=VV[:, hk, un["kb"], :], rhs=Wb[:, k, 1, :],
                                           start=False, stop=un["last"], skip_group_check=True)],
                        reads=[VVB[hk], WB[k]], writes=[psb[obank]])
                    if un["last"]:
                        yk = un["qi"] % 2
                        m = un["m"]
                        P.op("dve", lambda e: e.tensor_copy(out=yst[:, yk, :], in_=ps(obank)),
                             reads=[psb[obank]], writes=[ystB[yk]])
                        P.op("pool", lambda e: e.dma_start(out=yb_s[hd, :, m * 512:(m + 1) * 512], in_=yst[:, yk, :]),
                             reads=[ystB[yk]], writes=[], dsem=ystS[yk])

                for s in range(-3, U):
                    if 0 <= s + 3 < U:
                        qk1(s + 3)
                    if 0 <= s + 1 < U:
                        act_sp(s + 1)
                    if 0 <= s < U:
                        act_w(s)
                    if 0 <= s + 2 < U:
                        act_e(s + 2)
                    if 0 <= s + 1 < U:
                        cgroup(s + 1)
                    if 0 <= s < U:
                        wv(s)
                    if 0 <= s < U and units[s]["first"] and units[s]["m"] + 1 < NQG:
                        load_q(units[s]["m"] + 1)

            for hd in range(NH):
                do_head(hd, hd * NQG)
            P.wait_all("pool", [(s, s.count) for s in ystS])
            P.run_block()

        with ExitStack() as es2:
          if 'C' in PHASES:
            def sb2(name, shape, dt):
                return es2.enter_context(nc.sbuf_tensor("ph3_" + name, list(shape), dt))

            xs = sb2("xs", [128, 2, D], F32)
            xsB = bufs(2)
            xsS = [dsem(), dsem()]
            xoS = [dsem(), dsem()]
            xT = sb2("xT", [128, DC, 512], F32)
            xTB = bufs(DC)
            hT = sb2("hT", [128, DC, 512], BF16)
            hTB = bufs(DC)
            big = sb2("big", [128, FC, 512], BF16)
            bigB = bufs(FC)
            bigS = [dsem() for _ in range(6)]
            sgs = sb2("sgs", [128, 4, 512], F32)
            sgsB = bufs(4)
            sq = sb2("sq", [128, 2, 512], BF16)
            sqB = bufs(2)
            rsb = sb2("rsb", [128, 512], F32)
            rsbB = Buf()
            tmpf = sb2("tmpf", [128, 2, 512], F32)
            tmpfB = bufs(2)
            pin = sb2("pin", [128, 4, 256], F32)
            pinB = Buf()
            pinS = dsem()
            pbf = sb2("pbf", [128, 4, 256], BF16)
            pbfB = Buf()
            pT = sb2("pT", [128, 2, 512], BF16)
            pTB = bufs(2)
            ring = Ring(sb2, "ringC", 3)
            ring2 = sb2("ring2", [128, 2, 2, 512], BF16)
            ring2B = bufs(2)
            ring2S = [dsem(), dsem()]
            obank = [0]

            def mm_group(bank, s, j, KC, in_ap, inB, start, stop, kofs=0):
                P.group("pe", [lambda e, c=c: e.matmul(
                    ps(bank), lhsT=ring.t[:, s, c, j * 128:(j + 1) * 128], rhs=in_ap(kofs + c),
                    start=(start and c == 0), stop=(stop and c == KC - 1), skip_group_check=True) for c in range(KC)],
                    reads=list(inB) + [ring.B[s]], writes=[psb[bank]])

            def layer(key, wc, groups, KC, in_ap, inB, evac, nb=4):
                ci = 0
                for g in groups:
                    s = ring.load(key, wc, g, 0, KC)
                    for j in range(4):
                        bank = obank[0] % nb
                        obank[0] += 1
                        mm_group(bank, s, j, KC, in_ap, inB, True, True)
                        evac(ci, bank)
                        ci += 1

            def norm_fm(gi):
                for c in range(DC):
                    k = c % 2
                    P.op("act", lambda e, c=c, k=k: e.activation(out=sq[:, k, :], in_=xT[:, c, :], func=AF.Square),
                         reads=[xTB[c]], writes=[sqB[k]])
                    P.op("pe", lambda e, c=c, k=k: e.matmul(ps(4), lhsT=ones_b[:], rhs=sq[:, k, :], start=(c == 0),
                                                            stop=(c == DC - 1), skip_group_check=True),
                         reads=[sqB[k], B_const], writes=[psb[4]])
                rstd_from_ps(4, rsb[:], rsbB, D)
                for c in range(DC):
                    P.op("dve", lambda e, c=c: e.scalar_tensor_tensor(out=hT[:, c, :], in0=xT[:, c, :],
                                                                      scalar=gT[:, gi, c:c + 1], in1=rsb[:],
                                                                      op0=ALU.mult, op1=ALU.mult),
                         reads=[xTB[c], rsbB, B_const], writes=[hTB[c]])

            def c_tile(t):
                c0 = t * 512
                P.op("sp", lambda e: e.dma_start(out=big[:, 0:8, :], in_=yb_s[:, :, c0:c0 + 512].rearrange("h p t -> p h t")),
                     writes=bigB[0:8], dsem=bigS[0])
                if CSTOP < -2:
                    return
                P.op("sp", lambda e: e.dma_start(out=pin[:], in_=po[c0:c0 + 512, :].rearrange("(s p) f -> p s f", p=128)),
                     writes=[pinB], dsem=pinS)
                if CSTOP < -1:
                    return
                for sub in range(4):
                    k = sub % 2
                    P.op("sp", lambda e, sub=sub, k=k: e.dma_start(out=xs[:, k, :], in_=xo[c0 + sub * 128:c0 + (sub + 1) * 128, :]),
                         writes=[xsB[k]], dsem=xsS[k])
                    for c in range(DC):
                        bank = 4 + (c % 4)
                        P.op("pe", lambda e, c=c, k=k, bank=bank: e.transpose(
                            ps(bank, 128), xs[:, k, c * 128:(c + 1) * 128], ident_f[:]),
                            reads=[xsB[k], B_const], writes=[psb[bank]])
                        if c % 2 == 0:
                            P.op("dve", lambda e, c=c, sub=sub, bank=bank: e.tensor_copy(
                                out=xT[:, c, sub * 128:(sub + 1) * 128], in_=ps(bank, 128)),
                                reads=[psb[bank], xTB[c]], writes=[xTB[c]])
                        else:
                            P.op("act", lambda e, c=c, sub=sub, bank=bank: e.activation(
                                out=xT[:, c, sub * 128:(sub + 1) * 128], in_=ps(bank, 128),
                                func=AF.Copy), reads=[psb[bank], xTB[c]], writes=[xTB[c]])
                if CSTOP < 0:
                    return
                P.op("dve", lambda e: e.tensor_copy(out=pbf[:], in_=pin[:]), reads=[pinB], writes=[pbfB])
                for f in range(2):
                    P.group("pe", [lambda e, sub=sub, f=f: e.transpose(psbf(4 + f, 512)[:, sub * 128:(sub + 1) * 128],
                                                                       pbf[:, sub, f * 128:(f + 1) * 128], ident_b[:])
                                   for sub in range(4)], reads=[pbfB, B_const], writes=[psb[4 + f]])
                    P.op("dve", lambda e, f=f: e.tensor_copy(out=pT[:, f, :], in_=psbf(4 + f, 512)),
                         reads=[psb[4 + f]], writes=[pTB[f]])

                if CSTOP < 1:
                    return
                def ev_ub(ci, bank):
                    k = ci % 2
                    if ci % 4 == 0:
                        gi4 = (ci // 4) % 2
                        P.op("sp", lambda e: e.dma_start(out=big[:, 8 + 4 * gi4:12 + 4 * gi4, :],
                                                         in_=ga_s[ci:ci + 4, :, c0:c0 + 512].rearrange("h p t -> p h t")),
                             writes=bigB[8 + 4 * gi4:12 + 4 * gi4], dsem=bigS[1 + gi4])
                        P.op("sp", lambda e: e.dma_start(out=big[:, 16 + 4 * gi4:20 + 4 * gi4, :],
                                                         in_=sgb_s[ci:ci + 4, :, c0:c0 + 512].rearrange("h p t -> p h t")),
                             writes=bigB[16 + 4 * gi4:20 + 4 * gi4], dsem=bigS[3 + gi4])
                    ia = 8 + 4 * ((ci // 4) % 2) + ci % 4
                    ib = ia + 8
                    P.op("dve", lambda e: e.tensor_tensor(out=tmpf[:, k, :], in0=ps(bank), in1=big[:, ib, :], op=ALU.mult),
                         reads=[psb[bank], bigB[ib]], writes=[tmpfB[k]])
                    P.op("dve", lambda e: e.tensor_tensor(out=hT[:, ci, :], in0=tmpf[:, k, :], in1=big[:, ia, :],
                                                          op=ALU.add),
                         reads=[tmpfB[k], bigB[ia]], writes=[hTB[ci]])
                layer("ub", wc_ub, [0, 1, 2, 3], 8, lambda c: big[:, c, :], bigB[0:8], ev_ub)

                if CSTOP < 2:
                    return
                def ev_res(ci, bank):
                    P.op("dve", lambda e: e.tensor_tensor(out=xT[:, ci, :], in0=xT[:, ci, :], in1=ps(bank), op=ALU.add),
                         reads=[psb[bank], xTB[ci]], writes=[xTB[ci]])
                layer("o", wc_o, [0, 1, 2, 3], DC, lambda c: hT[:, c, :], hTB, ev_res)

                if CSTOP < 3:
                    return
                norm_fm(1)
                def ev_gate(ci, bank):
                    P.op("act", lambda e: e.activation(out=sgs[:, ci, :], in_=ps(bank), func=AF.Silu),
                         reads=[psb[bank]], writes=[sgsB[ci]])

                def mk_ev_up(gp):
                    def ev_up(ci, bank):
                        P.op("dve", lambda e: e.tensor_tensor(out=big[:, gp * 4 + ci, :], in0=ps(bank),
                                                              in1=sgs[:, ci, :], op=ALU.mult),
                             reads=[psb[bank], sgsB[ci]], writes=[bigB[gp * 4 + ci]])
                    return ev_up

                for gp in range(11):
                    layer("fi", wc_fi, [gp], DC, lambda c: hT[:, c, :], hTB, ev_gate)
                    layer("fi", wc_fi, [11 + gp], DC, lambda c: hT[:, c, :], hTB, mk_ev_up(gp))
                if CSTOP < 4:
                    return
                for g in range(4):
                    base = 4 * (g % 2)
                    parts = [(0, 16), (16, 32), (32, 44)]
                    for pi, (k0, k1) in enumerate(parts):
                        s = ring.load("fo", wc_fo, g, k0, k1)
                        for j in range(4):
                            mm_group(base + j, s, j, k1 - k0, lambda c: big[:, c, :], bigB[k0:k1], pi == 0,
                                     pi == len(parts) - 1, kofs=k0)
                    for j in range(4):
                        ev_res(g * 4 + j, base + j)

                if CSTOP < 5:
                    return
                norm_fm(2)
                for g in range(4):
                    s = ring.load("pg", wc_pg, g, 0, DC)
                    s2 = g % 2
                    P.op("sp", lambda e, g=g, s2=s2: e.dma_start(out=ring2[:, s2, :, :], in_=wc_pl[g, :, :, :]),
                         reads=[ready["pl"]], writes=[ring2B[s2]], dsem=ring2S[s2])
                    for j in range(4):
                        ci = g * 4 + j
                        ba = 2 * (ci % 2)
                        bb = ba + 1
                        k = ci % 2
                        mm_group(ba, s, j, DC, lambda c: hT[:, c, :], hTB, True, True)
                        P.group("pe", [lambda e, c=c, j=j, s2=s2, bb=bb: e.matmul(
                            ps(bb), lhsT=ring2[:, s2, c, j * 128:(j + 1) * 128], rhs=pT[:, c, :],
                            start=(c == 0), stop=(c == 1)) for c in range(2)],
                            reads=pTB + [ring2B[s2]], writes=[psb[bb]])
                        P.op("act", lambda e, ba=ba, k=k: e.activation(out=tmpf[:, k, :], in_=ps(ba), func=AF.Sigmoid),
                             reads=[psb[ba]], writes=[tmpfB[k]])
                        P.op("dve", lambda e, bb=bb, k=k: e.tensor_tensor(out=tmpf[:, k, :], in0=tmpf[:, k, :],
                                                                          in1=ps(bb), op=ALU.mult),
                             reads=[psb[bb], tmpfB[k]], writes=[tmpfB[k]])
                        P.op("dve", lambda e, ci=ci, k=k: e.tensor_tensor(out=xT[:, ci, :], in0=xT[:, ci, :],
                                                                          in1=tmpf[:, k, :], op=ALU.add),
                             reads=[tmpfB[k], xTB[ci]], writes=[xTB[ci]])

                if CSTOP < 6:
                    return
                for sub in range(4):
                    k = sub % 2
                    for c in range(DC):
                        bank = 4 + (c % 4)
                        P.op("pe", lambda e, c=c, sub=sub, bank=bank: e.transpose(
                            ps(bank, 128), xT[:, c, sub * 128:(sub + 1) * 128], ident_f[:]),
                            reads=[xTB[c], B_const], writes=[psb[bank]])
                        if c % 2 == 0:
                            P.op("dve", lambda e, c=c, k=k, bank=bank: e.tensor_copy(
                                out=xs[:, k, c * 128:(c + 1) * 128], in_=ps(bank, 128)),
                                reads=[psb[bank], xsB[k]], writes=[xsB[k]])
                        else:
                            P.op("act", lambda e, c=c, k=k, bank=bank: e.activation(
                                out=xs[:, k, c * 128:(c + 1) * 128], in_=ps(bank, 128), func=AF.Copy),
                                reads=[psb[bank], xsB[k]], writes=[xsB[k]])
                    r0 = c0 + sub * 128
                    P.op("pool", lambda e, k=k, r0=r0: e.dma_start(out=out[r0:r0 + 128, :], in_=xs[:, k, :]),
                         reads=[xsB[k]], writes=[], dsem=xoS[k])
            for t in range(NQG):
                c_tile(t)
            P.wait_all("pool", [(s, s.count) for s in xoS])
            P.run_block()
    return nc


_CACHE = {}


def _masks(i):
    j = np.arange(128)[:, None, None]
    r = np.arange(16)[None, :, None]
    t = np.arange(512)[None, None, :]
    ok = (r * 128 + j) < (4 * i * 128 + t)
    return np.where(ok, 0.0, NEG).astype(np.float32)


def kernel(**inputs):
    x = np.asarray(inputs["x"], dtype=np.float32)
    p = np.asarray(inputs["p"], dtype=np.float32)
    B, S, _ = x.shape
    NQG = S // 2048
    if NQG not in _CACHE:
        _CACHE[NQG] = build_program(NQG)
    nc = _CACHE[NQG]
    shared = {}
    for name in ("attn_norm_g", "w_in", "sgu_norm_g", "w_s", "b_s", "q_norm_g", "k_norm_g", "w_up_a", "w_up_b",
                 "w_o", "ffn_norm_g", "w_ffn_in", "w_ffn_out", "ple_norm_g", "w_ple_gate", "w_ple"):
        shared[name] = np.ascontiguousarray(np.asarray(inputs[name], dtype=np.float32)[0])
    in_maps = []
    idxs = []
    for core in range(8):
        b, i = divmod(core, 4)
        rows = np.concatenate([np.arange((4 * m + i) * 512, (4 * m + i + 1) * 512) for m in range(NQG)])
        idxs.append((b, rows))
        d = dict(shared)
        d["xb"] = np.ascontiguousarray(x[b])
        d["xo"] = np.ascontiguousarray(x[b][rows])
        d["po"] = np.ascontiguousarray(p[0, b][rows])
        d["maskd"] = _masks(i)
        in_maps.append(d)
    res = run_bass_kernel_spmd(nc, in_maps, core_ids=list(range(8)))
    outp = np.empty((B, S, D), dtype=np.float32)
    for core in range(8):
        b, rows = idxs[core]
        outp[b, rows] = res.results[core]["out"]
    if DEBUG:
        kernel.last = res
    return outp
```

```python
from contextlib import ExitStack

import numpy as np
import concourse.bass as bass
import concourse.mybir as mybir
from concourse.bass_utils import run_bass_kernel_spmd

F32 = mybir.dt.float32
BF16 = mybir.dt.bfloat16
AF = mybir.ActivationFunctionType
ALU = mybir.AluOpType

D = 2048
DC = 16
NH = 8
FH = 5632
FC = 44
INC = 9216
TT = 512
EPS = 1e-6
NEG = -30000.0
import os
DEBUG = bool(int(os.environ.get('KDEBUG', '0')))
PHASES = os.environ.get('KPHASES', 'KATC')
CSTOP = int(os.environ.get('KCSTOP', '99'))


class Sem:
    def __init__(self, h, step=1):
        self.h = h
        self.step = step
        self.count = 0


class Buf:
    __slots__ = ("w", "r")

    def __init__(self):
        self.w = None
        self.r = []


def bufs(n):
    return [Buf() for _ in range(n)]


class Prog:
    ENG = ("pe", "act", "dve", "pool", "sp")

    def __init__(self, nc, esem):
        self.nc = nc
        self.esem = esem
        self.q = {k: [] for k in self.ENG}
        self.waited = {k: {} for k in self.ENG}
        self.dsems = []

    def group(self, eng, fns, reads=(), writes=(), dsem=None):
        deps = {}

        def add(d):
            if d is None:
                return
            s, v = d
            if deps.get(s, 0) < v:
                deps[s] = v

        for b in reads:
            add(b.w)
        for b in writes:
            add(b.w)
            for d in b.r:
                add(d)
        waited = self.waited[eng]
        wl = []
        for s, v in deps.items():
            if eng in ("pe", "sp") and s is self.esem.get(eng):
                continue
            if waited.get(s, 0) >= v:
                continue
            waited[s] = v
            wl.append((s.h, v))
        sem = dsem if dsem is not None else self.esem[eng]
        sem.count += sem.step
        val = sem.count
        h, step = sem.h, sem.step

        def thunk(e):
            for (sh, v) in wl:
                e.wait_ge(sh, v)
            ins = None
            for f in fns:
                ins = f(e)
            ins.then_inc(h, step)

        self.q[eng].append(thunk)
        tok = (sem, val)
        for b in writes:
            b.w = tok
            b.r = []
        for b in reads:
            b.r.append(tok)
        return tok

    def op(self, eng, fn, reads=(), writes=(), dsem=None):
        return self.group(eng, [fn], reads, writes, dsem)

    def wait_all(self, eng, toks):
        best = {}
        for t in toks:
            if t is None:
                continue
            s, v = t
            if best.get(s, 0) < v:
                best[s] = v
        wl = [(s.h, v) for s, v in best.items()]

        def thunk(e):
            for (sh, v) in wl:
                e.wait_ge(sh, v)

        self.q[eng].append(thunk)

    def run_block(self):
        nc = self.nc
        q = self.q
        with nc.Block() as blk:
            @blk.tensor
            def _(e):
                for f in q["pe"]:
                    f(e)

            @blk.scalar
            def _(e):
                for f in q["act"]:
                    f(e)

            @blk.vector
            def _(e):
                for f in q["dve"]:
                    f(e)

            @blk.gpsimd
            def _(e):
                for f in q["pool"]:
                    f(e)

            @blk.sync
            def _(e):
                for f in q["sp"]:
                    f(e)
        self.q = {k: [] for k in self.ENG}


def build_program(NQG):
    S = 2048 * NQG
    NOWN = 512 * NQG
    NKT = S // TT
    NKB = S // 128
    nc = bass.Bass("TRN2", target_bir_lowering=False)

    def din(name, shape):
        return nc.dram_tensor(name, list(shape), F32, kind="ExternalInput").ap()

    xb = din("xb", [S, D])
    xo = din("xo", [NOWN, D])
    po = din("po", [NOWN, 256])
    maskd = din("maskd", [128, 16, 512])
    g_attn = din("attn_norm_g", [D])
    w_in = din("w_in", [D, INC])
    sgu_g = din("sgu_norm_g", [8, 128])
    w_s = din("w_s", [8, 128, 128])
    b_s = din("b_s", [8, 128])
    qg = din("q_norm_g", [128])
    kg = din("k_norm_g", [128])
    w_up_a = din("w_up_a", [1024, D])
    w_up_b = din("w_up_b", [1024, D])
    w_o = din("w_o", [D, D])
    g_ffn = din("ffn_norm_g", [D])
    w_fi = din("w_ffn_in", [D, 2 * FH])
    w_fo = din("w_ffn_out", [FH, D])
    g_ple = din("ple_norm_g", [D])
    w_pg = din("w_ple_gate", [D, D])
    w_ple = din("w_ple", [256, D])
    out = nc.dram_tensor("out", [NOWN, D], F32, kind="ExternalOutput").ap()

    skind = "ExternalOutput" if DEBUG else "Internal"

    def scratch(name, shape, dt=BF16):
        return nc.dram_tensor(name, list(shape), dt, kind=skind).ap()

    kT_s = scratch("kT_s", [NH, 128, S])
    v_s = scratch("v_s", [S, 1024])
    qT_s = scratch("qT_s", [NH, 128, NOWN])
    ga_s = scratch("ga_s", [DC, 128, NOWN])
    sgb_s = scratch("sgb_s", [DC, 128, NOWN])
    yb_s = scratch("yb_s", [NH, 128, NOWN])

    def wcache(name, K, N):
        return nc.dram_tensor(name, [N // 512, 128, K // 128, 512], BF16, kind="Internal").ap()

    wc_in = wcache("wc_in", D, INC)
    wc_ua = wcache("wc_ua", 1024, D)
    wc_ub = wcache("wc_ub", 1024, D)
    wc_o = wcache("wc_o", D, D)
    wc_fi = wcache("wc_fi", D, 2 * FH)
    wc_fo = wcache("wc_fo", FH, D)
    wc_pg = wcache("wc_pg", D, D)
    wc_pl = wcache("wc_pl", 256, D)

    with ExitStack() as es:
        def sb(name, shape, dt):
            return es.enter_context(nc.sbuf_tensor(name, list(shape), dt))

        def newsem(name, step=1):
            return Sem(es.enter_context(nc.semaphore(name)), step)

        esem = {k: newsem("e_" + k) for k in ("pe", "act", "dve", "pool")}
        esem["sp"] = newsem("e_sp_unused")
        P = Prog(nc, esem)
        nds = [0]

        def dsem():
            nds[0] += 1
            return newsem("d%d" % nds[0], 16)

        ident_f = sb("ident_f", [128, 128], F32)
        ident_b = sb("ident_b", [128, 128], BF16)
        ones_b = sb("ones_b", [128, 128], BF16)
        negones = sb("negones", [128, 128], BF16)
        negtri = sb("negtri", [128, 128], BF16)
        gT = sb("gT", [128, 3, DC], F32)
        sgT = sb("sgT", [128, 8], F32)
        gq = sb("gq", [128, 2], F32)
        wsT = sb("wsT", [128, 8, 128], BF16)
        bsb = sb("bsb", [128, 1024], F32)
        wtmp = sb("wtmp", [128, 128], F32)
        wtmp2 = sb("wtmp2", [128, 128], F32)
        psum = es.enter_context(nc.psum_tensor("psum", [128, 8, 512], F32))
        psb = [Buf() for _ in range(8)]

        def ps(i, n=512):
            return psum[:, i, 0:n]

        def psbf(i, n):
            return psum[:, i, :].bitcast(BF16)[:, 0:n]

        B_const = Buf()
        gstage = sb("gstage", [16, 4, 128], F32)
        B_gst = bufs(4)
        d_gst = [dsem() for _ in range(4)]
        B_wtmp = Buf()
        B_wtmp2 = Buf()
        d_const = dsem()
        d_wtmp = dsem()

        ready = {}

        cast_jobs = []
        cast_sems = {}

        def cast_weight(key, wc, w, K, N, groups=None, after=()):
            KC = K // 128
            G = N // 512
            for g in (groups if groups is not None else range(G)):
                for k0 in range(0, KC, 16):
                    k1 = min(KC, k0 + 16)
                    src = w[k0 * 128:k1 * 128, g * 512:(g + 1) * 512].rearrange("(kc p) j -> p kc j", p=128)
                    dst = wc[g, :, k0:k1, :]
                    cast_jobs.append((key, src, dst))

        def emit_casts(n):
            for _ in range(min(n, len(cast_jobs))):
                key, src, dst = cast_jobs.pop(0)
                if key not in cast_sems:
                    cast_sems[key] = dsem()
                    ready[key] = Buf()
                tok = P.op("pool", lambda e, s=src, d=dst: e.dma_start(out=d, in_=s), dsem=cast_sems[key])
                ready[key].w = tok

        def setup_consts():
            P.op("pool", lambda e: e.memset(ident_f[:], 1.0), writes=[B_const])
            P.op("pool", lambda e: e.affine_select(out=ident_f[:], in_=ident_f[:], pattern=[[1, 128]],
                                                   compare_op=ALU.is_equal, fill=0.0, base=0,
                                                   channel_multiplier=-1), writes=[B_const])
            P.op("pool", lambda e: e.tensor_copy(out=ident_b[:], in_=ident_f[:]), reads=[B_const], writes=[B_const])
            P.op("pool", lambda e: e.memset(ones_b[:], 1.0), writes=[B_const])
            P.op("pool", lambda e: e.memset(negones[:], -1.0), writes=[B_const])
            P.op("pool", lambda e: e.memset(negtri[:], -1.0), writes=[B_const])
            P.op("pool", lambda e: e.affine_select(out=negtri[:], in_=negtri[:], pattern=[[-1, 128]],
                                                   compare_op=ALU.is_ge, fill=0.0, base=0,
                                                   channel_multiplier=1), writes=[B_const])
            for i, g in enumerate((g_attn, g_ffn, g_ple, sgu_g)):
                rows = 8 if i == 3 else DC
                src = g if i == 3 else g.rearrange("(c p) -> c p", p=128)
                P.op("sp", lambda e, i=i, src=src, rows=rows: e.dma_start(out=gstage[0:rows, i, :], in_=src),
                     writes=[B_gst[i]], dsem=d_gst[i])
                P.op("pe", lambda e, i=i, rows=rows: e.transpose(ps(7, rows), gstage[0:rows, i, :],
                                                                 ident_f[0:rows, 0:rows]),
                     reads=[B_gst[i], B_const], writes=[psb[7]])
                dst = sgT[:] if i == 3 else gT[:, i, :]
                P.op("dve", lambda e, dst=dst, rows=rows: e.tensor_copy(out=dst, in_=ps(7, rows)),
                     reads=[psb[7]], writes=[B_const])
            P.op("sp", lambda e: e.dma_start(out=gq[:, 0:1], in_=qg.rearrange("(p o) -> p o", o=1)),
                 writes=[B_const], dsem=d_const)
            P.op("sp", lambda e: e.dma_start(out=gq[:, 1:2], in_=kg.rearrange("(p o) -> p o", o=1)),
                 writes=[B_const], dsem=d_const)
            P.op("sp", lambda e: e.dma_start(out=bsb[:], in_=b_s.rearrange("g t -> (g t)").partition_broadcast(128)),
                 writes=[B_const], dsem=d_const)
            P.op("dve", lambda e: e.scalar_tensor_tensor(out=gq[:, 0:1], in0=gq[:, 0:1], scalar=float(128 ** -0.5),
                                                         in1=gq[:, 1:2], op0=ALU.mult, op1=ALU.mult),
                 reads=[B_const], writes=[B_const])
            for g in range(8):
                P.op("sp", lambda e, g=g: e.dma_start(out=wtmp[:], in_=w_s[g]), writes=[B_wtmp], dsem=d_wtmp)
                P.op("pe", lambda e: e.transpose(ps(7, 128), wtmp[:], ident_f[:]), reads=[B_wtmp, B_const],
                     writes=[psb[7]])
                P.op("dve", lambda e: e.tensor_copy(out=wtmp2[:], in_=ps(7, 128)), reads=[psb[7]], writes=[B_wtmp2])
                P.op("pool", lambda e, g=g: e.affine_select(out=wsT[:, g, :], in_=wtmp2[:], pattern=[[1, 128]],
                                                            compare_op=ALU.is_ge, fill=0.0, base=0,
                                                            channel_multiplier=-1),
                     reads=[B_wtmp2], writes=[B_const])

        def rstd_from_ps(bank, dst, dstB, nfeat):
            P.op("act", lambda e: e.activation(out=dst, in_=ps(bank), func=AF.Ln, bias=EPS, scale=1.0 / nfeat),
                 reads=[psb[bank]], writes=[dstB])
            P.op("act", lambda e: e.activation(out=dst, in_=dst, func=AF.Exp, scale=-0.5),
                 reads=[dstB], writes=[dstB])

        class Ring:
            def __init__(self, alloc, name, nslots):
                self.t = alloc(name, [128, nslots, 16, 512], BF16)
                self.B = bufs(nslots)
                self.sem = [dsem() for _ in range(nslots)]
                self.n = nslots
                self.i = 0

            def load(self, key, wc, g, k0, k1):
                s = self.i % self.n
                self.i += 1
                cb = ready[key]
                P.op("sp", lambda e, s=s: e.dma_start(out=self.t[:, s, 0:k1 - k0, :], in_=wc[g, :, k0:k1, :]),
                     reads=[cb], writes=[self.B[s]], dsem=self.sem[s])
                return s

        def norm_tm(src_rows, gi, xs, xsB, xsS, xn, xnB, junk, junkB, ss, ssB, hT, hTB, trbank, part="both"):
            if part in ("both", "pre"):
                norm_tm_pre(src_rows, xs, xsB, xsS, xn, xnB, junk, junkB, ss, ssB)
            if part in ("both", "post"):
                norm_tm_post(gi, xn, xnB, hT, hTB, trbank)

        def norm_tm_pre(src_rows, xs, xsB, xsS, xn, xnB, junk, junkB, ss, ssB):
            P.op("dve", lambda e: e.memset(ss[:], 0.0), writes=[ssB])
            for sub in range(4):
                k = sub % 2
                P.op("sp", lambda e, sub=sub, k=k: e.dma_start(out=xs[:, k, :], in_=src_rows[sub * 128:(sub + 1) * 128, :]),
                     writes=[xsB[k]], dsem=xsS[k])
                P.op("act", lambda e, sub=sub, k=k: e.activation(out=junk[:], in_=xs[:, k, :], func=AF.Square,
                                                                 accum_out=ss[:, sub:sub + 1]),
                     reads=[xsB[k]], writes=[junkB, ssB])
                P.op("act", lambda e, sub=sub: e.activation(out=ss[:, 4 + sub:5 + sub], in_=ss[:, sub:sub + 1],
                                                            func=AF.Ln, bias=EPS, scale=1.0 / D),
                     reads=[ssB], writes=[ssB])
                P.op("act", lambda e, sub=sub: e.activation(out=ss[:, 4 + sub:5 + sub], in_=ss[:, 4 + sub:5 + sub],
                                                            func=AF.Exp, scale=-0.5),
                     reads=[ssB], writes=[ssB])
                P.op("dve", lambda e, sub=sub, k=k: e.tensor_scalar(out=xn[:, sub, :], in0=xs[:, k, :],
                                                                    scalar1=ss[:, 4 + sub:5 + sub], scalar2=None,
                                                                    op0=ALU.mult),
                     reads=[xsB[k], ssB], writes=[xnB[sub]])

        def norm_tm_post(gi, xn, xnB, hT, hTB, trbank):
            for c in range(DC):
                bank = trbank[c % len(trbank)]
                P.group("pe", [lambda e, c=c, sub=sub, bank=bank: e.transpose(
                    psbf(bank, 512)[:, sub * 128:(sub + 1) * 128], xn[:, sub, c * 128:(c + 1) * 128], ident_b[:])
                    for sub in range(4)], reads=xnB + [B_const], writes=[psb[bank]])
                if c % 2 == 0:
                    P.op("dve", lambda e, c=c, bank=bank: e.tensor_scalar(out=hT[:, c, :], in0=psbf(bank, 512),
                                                                          scalar1=gT[:, gi, c:c + 1], scalar2=None,
                                                                          op0=ALU.mult),
                         reads=[psb[bank], B_const], writes=[hTB[c]])
                else:
                    P.op("act", lambda e, c=c, bank=bank: e.activation(out=hT[:, c, :], in_=psbf(bank, 512),
                                                                       func=AF.Copy, scale=gT[:, gi, c:c + 1]),
                         reads=[psb[bank], B_const], writes=[hTB[c]])

        with ExitStack() as es2:
          if 'K' in PHASES:
            def sb2(name, shape, dt):
                return es2.enter_context(nc.sbuf_tensor("ph0_" + name, list(shape), dt))

            wkv = sb2("wkv", [128, 4, 16, 512], BF16)
            wkvB = bufs(4)
            xs = sb2("xs", [128, 2, D], F32)
            xsB = bufs(2)
            xsS = [dsem(), dsem()]
            xn = sb2("xn", [128, 4, D], BF16)
            xnB = bufs(4)
            junk = sb2("junk", [128, D], BF16)
            junkB = Buf()
            ss = sb2("ss", [128, 8], F32)
            ssB = Buf()
            hT2 = sb2("hT", [128, 2, DC, 512], BF16)
            hTB2 = [bufs(DC), bufs(DC)]
            sqk = sb2("sqk", [128, 2, 512], BF16)
            sqkB = bufs(2)
            rsb = sb2("rsb", [128, 2, 512], F32)
            rsbB = bufs(2)
            kst = sb2("kst", [128, 2, 512], BF16)
            kstB = bufs(2)
            kstS = [dsem(), dsem()]
            vst = sb2("vst", [128, 2, 1024], BF16)
            vstB = bufs(2)
            vstS = [dsem(), dsem()]
            d_wkv = [dsem() for _ in range(4)]

            wstage = sb2("wstage", [128, 16, 512], F32)
            wstageB = Buf()
            d_wst = dsem()
            for i, g in enumerate((6, 7, 8, 9)):
                P.op("sp", lambda e, g=g: e.dma_start(
                    out=wstage[:], in_=w_in[:, g * 512:(g + 1) * 512].rearrange("(kc p) j -> p kc j", p=128)),
                    writes=[wstageB], dsem=d_wst)
                P.op("dve", lambda e, i=i: e.tensor_copy(out=wkv[:, i, 0:8, :], in_=wstage[:, 0:8, :]),
                     reads=[wstageB], writes=[wkvB[i]])
                P.op("act", lambda e, i=i: e.activation(out=wkv[:, i, 8:16, :], in_=wstage[:, 8:16, :], func=AF.Copy),
                     reads=[wstageB, wkvB[i]], writes=[wkvB[i]])
            setup_consts()
            cast_weight("in", wc_in, w_in, D, INC, groups=[0, 1, 2, 3, 4, 5] + list(range(10, 18)), after=wkvB)
            cast_weight("ua", wc_ua, w_up_a, 1024, D)
            cast_weight("ub", wc_ub, w_up_b, 1024, D)
            cast_weight("o", wc_o, w_o, D, D)
            cast_weight("fi", wc_fi, w_fi, D, 2 * FH)
            cast_weight("fo", wc_fo, w_fo, FH, D)
            cast_weight("pg", wc_pg, w_pg, D, D)
            cast_weight("pl", wc_pl, w_ple, 256, D)
            ukv = [0, 0]

            def kv_norm(t, part):
                norm_tm(xb[t * 512:(t + 1) * 512, :], 0, xs, xsB, xsS, xn, xnB, junk, junkB, ss, ssB,
                        hT2[:, t % 2], hTB2[t % 2], [6, 7], part=part)

            def kv_tile(t):
                uk, uv = ukv
                hT = hT2[:, t % 2]
                hTB = hTB2[t % 2]
                if t == 0:
                    kv_norm(0, "both")
                def k_main(hd, k):
                    bank = (uk + hd) % 3
                    wi, wo = divmod(hd * 128, 512)
                    P.group("pe", [lambda e, c=c: e.matmul(
                        ps(bank), lhsT=wkv[:, wi, c, wo:wo + 128], rhs=hT[:, c, :], start=(c == 0), stop=(c == DC - 1))
                        for c in range(DC)], reads=hTB + [wkvB[wi]], writes=[psb[bank]])
                    P.op("act", lambda e: e.activation(out=sqk[:, k, :], in_=ps(bank), func=AF.Square),
                         reads=[psb[bank]], writes=[sqkB[k]])

                def k_tail(hd, k):
                    bank = (uk + hd) % 3
                    sbank = 3
                    P.op("pe", lambda e: e.matmul(ps(sbank), lhsT=ones_b[:], rhs=sqk[:, k, :], start=True, stop=True),
                         reads=[sqkB[k], B_const], writes=[psb[sbank]])
                    rstd_from_ps(sbank, rsb[:, k, :], rsbB[k], 128)
                    P.op("dve", lambda e: e.tensor_tensor(out=kst[:, k, :], in0=ps(bank), in1=rsb[:, k, :], op=ALU.mult),
                         reads=[psb[bank], rsbB[k]], writes=[kstB[k]])
                    P.op("pool", lambda e: e.dma_start(out=kT_s[hd, :, t * 512:(t + 1) * 512], in_=kst[:, k, :]),
                         reads=[kstB[k]], writes=[], dsem=kstS[k])

                for hd in range(NH + 1):
                    if hd == 2 and t + 1 < NKT:
                        kv_norm(t + 1, "pre")
                    if hd < NH:
                        k_main(hd, (uk + hd) % 2)
                    if hd >= 1:
                        k_tail(hd - 1, (uk + hd - 1) % 2)
                uk += NH
                if t + 1 < NKT:
                    kv_norm(t + 1, "post")
                for sub in range(4):
                    k = uv % 2
                    uv += 1
                    for cg in range(2):
                        bank = 4 + cg
                        P.group("pe", [lambda e, c=c, sub=sub, cg=cg, bank=bank: e.matmul(
                            ps(bank), lhsT=hT[:, c, sub * 128:(sub + 1) * 128], rhs=wkv[:, 2 + cg, c, :],
                            start=(c == 0), stop=(c == DC - 1)) for c in range(DC)],
                            reads=hTB + [wkvB[2 + cg]], writes=[psb[bank]])
                        if cg == 0:
                            P.op("dve", lambda e, k=k, bank=bank: e.tensor_copy(out=vst[:, k, 0:512], in_=ps(bank)),
                                 reads=[psb[bank]], writes=[vstB[k]])
                        else:
                            P.op("act", lambda e, k=k, bank=bank: e.activation(out=vst[:, k, 512:1024], in_=ps(bank),
                                                                               func=AF.Copy),
                                 reads=[psb[bank], vstB[k]], writes=[vstB[k]])
                    r0 = t * 512 + sub * 128
                    P.op("pool", lambda e, k=k, r0=r0: e.dma_start(out=v_s[r0:r0 + 128, :], in_=vst[:, k, :]),
                         reads=[vstB[k]], writes=[], dsem=vstS[k])
                ukv[0], ukv[1] = uk, uv

            n_early = 18
            per_tile = (n_early + NKT - 2) // max(1, NKT - 1)
            left = n_early
            for t in range(NKT):
                kv_tile(t)
                n = min(per_tile, left)
                emit_casts(n)
                left -= n
            emit_casts(left)
            P.wait_all("pool", [(s, s.count) for s in kstS + vstS])
            P.run_block()

        with ExitStack() as es2:
          if 'A' in PHASES:
            def sb2(name, shape, dt):
                return es2.enter_context(nc.sbuf_tensor("ph1_" + name, list(shape), dt))

            ring = None
            xs = sb2("xs", [128, 2, D], F32)
            xsB = bufs(2)
            xsS = [dsem(), dsem()]
            xn = sb2("xn", [128, 4, D], BF16)
            xnB = bufs(4)
            junk = sb2("junk", [128, D], BF16)
            junkB = Buf()
            ss = sb2("ss", [128, 8], F32)
            ssB = Buf()
            hT2 = sb2("hT", [128, 2, DC, 512], BF16)
            hTB2 = [bufs(DC), bufs(DC)]
            u_sb = sb2("u_sb", [128, 8, 512], BF16)
            uB = bufs(8)
            ya = sb2("ya", [128, 8, 512], BF16)
            yaB = bufs(8)
            sga = sb2("sga", [128, DC, 512], BF16)
            sgaB = bufs(DC)
            gv = sb2("gv", [128, 4, 512], F32)
            gvB = bufs(4)
            sq = sb2("sq", [128, 4, 512], BF16)
            sqB = bufs(4)
            rsb = sb2("rsb", [128, 4, 512], F32)
            rsbB = bufs(4)
            vn = sb2("vn", [128, 2, 512], BF16)
            vnB = bufs(2)
            vtok = sb2("vtok", [128, 2, 512], BF16)
            vtokB = bufs(2)
            tmpf = sb2("tmpf", [128, 2, 512], F32)
            tmpfB = bufs(2)
            stg = sb2("stg", [128, 4, 512], BF16)
            stgB = bufs(4)
            stgS = [dsem() for _ in range(4)]
            ring = Ring(sb2, "ringA", 3)
            stg_i = [0]

            def store_stage(dst_ap_fn, producer):
                k = stg_i[0] % 4
                stg_i[0] += 1
                producer(k)
                P.op("pool", lambda e, k=k: e.dma_start(out=dst_ap_fn(), in_=stg[:, k, :]),
                     reads=[stgB[k]], writes=[], dsem=stgS[k])

            obank = [0]

            pending = []

            def run_pending():
                for st in list(pending):
                    st.pop(0)()
                    if not st:
                        pending.remove(st)

            def flush_pending():
                while pending:
                    run_pending()

            def layer(key, wc, groups, KC, in_ap, inB, evac):
                ci = 0
                for g in groups:
                    s = ring.load(key, wc, g, 0, KC)
                    for j in range(4):
                        bank = obank[0] % 4
                        obank[0] += 1
                        P.group("pe", [lambda e, c=c, s=s, j=j, bank=bank: e.matmul(
                            ps(bank), lhsT=ring.t[:, s, c, j * 128:(j + 1) * 128], rhs=in_ap(c),
                            start=(c == 0), stop=(c == KC - 1)) for c in range(KC)],
                            reads=list(inB) + [ring.B[s]], writes=[psb[bank]])
                        run_pending()
                        st = evac(ci, bank)
                        if st:
                            pending.append(list(st))
                        ci += 1

            def a_norm(t, part):
                norm_tm(xo[t * 512:(t + 1) * 512, :], 0, xs, xsB, xsS, xn, xnB, junk, junkB, ss, ssB,
                        hT2[:, t % 2], hTB2[t % 2], [6, 7], part=part)

            def a_tile(t):
                c0 = t * 512
                hT = hT2[:, t % 2]
                hTB = hTB2[t % 2]
                if t == 0:
                    a_norm(0, "both")

                def ev_u(ci, bank):
                    P.op("act", lambda e: e.activation(out=u_sb[:, ci, :], in_=ps(bank), func=AF.Gelu),
                         reads=[psb[bank]], writes=[uB[ci]])
                layer("in", wc_in, [0, 1], DC, lambda c: hT[:, c, :], hTB, ev_u)

                def ev_v(ci, bank):
                    k = ci % 4
                    k2 = ci % 2
                    P.op("act", lambda e: e.activation(out=gv[:, k, :], in_=ps(bank), func=AF.Gelu),
                         reads=[psb[bank]], writes=[gvB[k]])
                    P.op("act", lambda e: e.activation(out=sq[:, k, :], in_=gv[:, k, :], func=AF.Square),
                         reads=[gvB[k]], writes=[sqB[k]])

                    def s1():
                        P.op("pe", lambda e: e.matmul(ps(4), lhsT=ones_b[:], rhs=sq[:, k, :], start=True, stop=True),
                             reads=[sqB[k], B_const], writes=[psb[4]])
                        rstd_from_ps(4, rsb[:, k, :], rsbB[k], 128)
                        P.op("dve", lambda e: e.scalar_tensor_tensor(out=vn[:, k2, :], in0=gv[:, k, :],
                                                                     scalar=sgT[:, ci:ci + 1], in1=rsb[:, k, :],
                                                                     op0=ALU.mult, op1=ALU.mult),
                             reads=[gvB[k], rsbB[k], B_const], writes=[vnB[k2]])

                    def s2():
                        P.group("pe", [lambda e, sub=sub: e.transpose(psbf(5, 512)[:, sub * 128:(sub + 1) * 128],
                                                                      vn[:, k2, sub * 128:(sub + 1) * 128], ident_b[:])
                                       for sub in range(4)], reads=[vnB[k2], B_const], writes=[psb[5]])
                        P.op("dve", lambda e: e.tensor_copy(out=vtok[:, k2, :], in_=psbf(5, 512)),
                             reads=[psb[5]], writes=[vtokB[k2]])

                    def s3():
                        P.group("pe", [lambda e, sub=sub: e.matmul(ps(6)[:, sub * 128:(sub + 1) * 128],
                                                                   lhsT=vtok[:, k2, sub * 128:(sub + 1) * 128],
                                                                   rhs=wsT[:, ci, :], start=True, stop=True)
                                       for sub in range(4)], reads=[vtokB[k2], B_const], writes=[psb[6]])
                        P.group("dve", [lambda e, sub=sub: e.tensor_tensor(
                            out=tmpf[:, k2, sub * 128:(sub + 1) * 128], in0=ps(6)[:, sub * 128:(sub + 1) * 128],
                            in1=bsb[:, ci * 128:(ci + 1) * 128], op=ALU.add) for sub in range(4)],
                            reads=[psb[6], B_const], writes=[tmpfB[k2]])
                        P.op("dve", lambda e: e.tensor_tensor(out=ya[:, ci, :], in0=tmpf[:, k2, :], in1=u_sb[:, ci, :],
                                                              op=ALU.mult),
                             reads=[tmpfB[k2], uB[ci]], writes=[yaB[ci]])
                    return [s1, s2, s3]
                layer("in", wc_in, [2, 3], DC, lambda c: hT[:, c, :], hTB, ev_v)

                def ev_q(ci, bank):
                    k = ci % 4
                    P.op("act", lambda e: e.activation(out=sq[:, k, :], in_=ps(bank), func=AF.Square),
                         reads=[psb[bank]], writes=[sqB[k]])

                    def s1():
                        P.op("pe", lambda e: e.matmul(ps(4), lhsT=ones_b[:], rhs=sq[:, k, :], start=True, stop=True),
                             reads=[sqB[k], B_const], writes=[psb[4]])
                        rstd_from_ps(4, rsb[:, k, :], rsbB[k], 128)
                        store_stage(lambda: qT_s[ci, :, c0:c0 + 512],
                                    lambda kk: P.op("dve", lambda e: e.scalar_tensor_tensor(
                                        out=stg[:, kk, :], in0=ps(bank), scalar=gq[:, 0:1], in1=rsb[:, k, :],
                                        op0=ALU.mult, op1=ALU.mult), reads=[psb[bank], rsbB[k], B_const],
                                        writes=[stgB[kk]]))
                    return [s1]
                layer("in", wc_in, [4, 5], DC, lambda c: hT[:, c, :], hTB, ev_q)

                if t + 1 < NQG:
                    a_norm(t + 1, "pre")
                def ev_ga(ci, bank):
                    P.op("act", lambda e: e.activation(out=sga[:, ci, :], in_=ps(bank), func=AF.Sigmoid),
                         reads=[psb[bank]], writes=[sgaB[ci]])
                layer("in", wc_in, [10, 11, 12, 13], DC, lambda c: hT[:, c, :], hTB, ev_ga)

                if t + 1 < NQG:
                    a_norm(t + 1, "post")
                def ev_gb(ci, bank):
                    store_stage(lambda: sgb_s[ci, :, c0:c0 + 512],
                                lambda kk: P.op("act", lambda e: e.activation(out=stg[:, kk, :], in_=ps(bank),
                                                                               func=AF.Sigmoid),
                                                reads=[psb[bank]], writes=[stgB[kk]]))
                layer("in", wc_in, [14, 15, 16, 17], DC, lambda c: hT[:, c, :], hTB, ev_gb)

                def ev_ua(ci, bank):
                    store_stage(lambda: ga_s[ci, :, c0:c0 + 512],
                                lambda kk: P.op("dve", lambda e: e.tensor_tensor(out=stg[:, kk, :], in0=ps(bank),
                                                                                 in1=sga[:, ci, :], op=ALU.mult),
                                                reads=[psb[bank], sgaB[ci]], writes=[stgB[kk]]))
                flush_pending()
                layer("ua", wc_ua, [0, 1, 2, 3], 8, lambda c: ya[:, c, :], yaB, ev_ua)
                flush_pending()

            per_tile_a = (len(cast_jobs) + NQG - 1) // NQG
            for t in range(NQG):
                a_tile(t)
                emit_casts(per_tile_a)
            emit_casts(len(cast_jobs))
            P.wait_all("pool", [(s, s.count) for s in stgS])
            P.run_block()

        with ExitStack() as es2:
          if 'T' in PHASES:
            def sb2(name, shape, dt):
                return es2.enter_context(nc.sbuf_tensor("ph2_" + name, list(shape), dt))

            masks = sb2("masks", [128, 16, 512], BF16)
            maskB = Buf()
            d_mask = dsem()
            KT = sb2("KT", [128, 2, S], BF16)
            KTB = bufs(2)
            KTS = [dsem(), dsem()]
            VV = sb2("VV", [128, 2, NKB, 128], BF16)
            VVB = bufs(2)
            VVS = [dsem(), dsem()]
            qT = sb2("qT", [128, 2, 512], BF16)
            qTB = bufs(2)
            qTS = [dsem(), dsem()]
            Eb = sb2("Eb", [128, 3, 2, 512], F32)
            EB = bufs(3)
            XC = sb2("XC", [128, 2, 2, 512], F32)
            XCB = bufs(2)
            SPs = sb2("SPs", [128, 2, 512], BF16)
            SPsB = bufs(2)
            SPt = sb2("SPt", [128, 2, 512], BF16)
            SPtB = bufs(2)
            SPb = sb2("SPb", [128, 2, 2, 512], BF16)
            SPB = bufs(2)
            Wb = sb2("Wb", [128, 2, 2, 512], BF16)
            WB = bufs(2)
            yst = sb2("yst", [128, 2, 512], BF16)
            ystB = bufs(2)
            ystS = [dsem(), dsem()]

            P.op("pool", lambda e: e.dma_start(out=masks[:], in_=maskd), writes=[maskB], dsem=d_mask)

            def load_head(hd):
                k = hd % 2
                nsplit = max(1, S // 4096)
                for i in range(nsplit):
                    a, b = i * (S // nsplit), (i + 1) * (S // nsplit)
                    P.op("sp", lambda e, a=a, b=b, k=k: e.dma_start(out=KT[:, k, a:b], in_=kT_s[hd, :, a:b]),
                         writes=[KTB[k]], dsem=KTS[k])
                vsrc = v_s.rearrange("(b p) c -> p b c", p=128)
                for b0 in range(0, NKB, 16):
                    P.op("sp", lambda e, b0=b0, k=k: e.dma_start(out=VV[:, k, b0:b0 + 16, :],
                                                                 in_=vsrc[:, b0:b0 + 16, hd * 128:(hd + 1) * 128]),
                         writes=[VVB[k]], dsem=VVS[k])

            load_head(0)
            def do_head(hd, gq_i):
                hk = hd % 2
                if hd + 1 < NH:
                    load_head(hd + 1)
                units = []
                for m in range(NQG):
                    nk = 16 * m + 16
                    for idx in range(nk // 2):
                        ka = nk - 1 - 2 * idx
                        kb_ = ka - 1
                        units.append(dict(m=m, ka=ka, kb=kb_, first=(idx == 0), last=(idx == nk // 2 - 1),
                                          ra=(ka - 16 * m) if ka >= 16 * m else None,
                                          rb=(kb_ - 16 * m) if kb_ >= 16 * m else None, qi=gq_i + m))
                U = len(units)

                def load_q(m):
                    qk = (gq_i + m) % 2
                    P.op("sp", lambda e: e.dma_start(out=qT[:, qk, :], in_=qT_s[hd, :, m * 512:(m + 1) * 512]),
                         writes=[qTB[qk]], dsem=qTS[qk])
                load_q(0)

                def qk1(u):
                    un = units[u]
                    qk = un["qi"] % 2
                    zb = 2 * (u % 2)
                    fns = []
                    rd = [KTB[hk], qTB[qk]]
                    for half, (kk, rr) in enumerate(((un["ka"], un["ra"]), (un["kb"], un["rb"]))):
                        fns.append(lambda e, kk=kk, rr=rr, half=half: e.matmul(
                            ps(zb + half), lhsT=KT[:, hk, kk * 128:(kk + 1) * 128], rhs=qT[:, qk, :],
                            start=True, stop=(rr is None)))
                        if rr is not None:
                            fns.append(lambda e, rr=rr, half=half: e.matmul(
                                ps(zb + half), lhsT=ident_b[:], rhs=masks[:, rr, :], start=False, stop=True))
                            rd += [maskB, B_const]
                    P.group("pe", fns, reads=rd, writes=[psb[zb], psb[zb + 1]])

                def act_e(u):
                    zb = 2 * (u % 2)
                    ek = u % 3
                    P.op("act", lambda e: e.activation(out=Eb[:, ek, :, :], in_=psum[:, zb:zb + 2, :], func=AF.Exp),
                         reads=[psb[zb], psb[zb + 1]], writes=[EB[ek]])

                def act_sp(u):
                    k = u % 2
                    ek = u % 3
                    un = units[u]
                    P.op("act", lambda e: e.activation(out=SPb[:, k, :, :], in_=Eb[:, ek, :, :], func=AF.Ln, bias=1.0,
                                                       scale=1.0),
                         reads=[EB[ek]], writes=[SPB[k]])
                    if not un["last"]:
                        if un["first"]:
                            P.op("dve", lambda e: e.tensor_tensor(out=SPs[:, k, :], in0=SPb[:, k, 0, :],
                                                                  in1=SPb[:, k, 1, :], op=ALU.add),
                                 reads=[SPB[k]], writes=[SPsB[k]])
                        else:
                            P.op("dve", lambda e: e.tensor_tensor(out=SPt[:, k, :], in0=SPb[:, k, 0, :],
                                                                  in1=SPb[:, k, 1, :], op=ALU.add),
                                 reads=[SPB[k]], writes=[SPtB[k]])
                            P.op("pool", lambda e: e.tensor_tensor(out=SPs[:, k, :], in0=SPs[:, 1 - k, :],
                                                                   in1=SPt[:, k, :], op=ALU.add),
                                 reads=[SPtB[k], SPsB[1 - k]], writes=[SPsB[k]])

                def cgroup(u):
                    un = units[u]
                    k = u % 2
                    first = un["first"]
                    fns = [lambda e: e.matmul(ps(4), lhsT=negtri[:], rhs=SPb[:, k, 0, :], start=True, stop=first)]
                    if not first:
                        fns.append(lambda e: e.matmul(ps(4), lhsT=negones[:], rhs=SPs[:, 1 - k, :], start=False,
                                                      stop=True))
                    fns.append(lambda e: e.matmul(ps(5), lhsT=negtri[:], rhs=SPb[:, k, 1, :], start=True, stop=False))
                    fns.append(lambda e: e.matmul(ps(5), lhsT=negones[:], rhs=SPb[:, k, 0, :], start=False,
                                                  stop=first))
                    rd = [SPB[k], B_const]
                    if not first:
                        fns.append(lambda e: e.matmul(ps(5), lhsT=negones[:], rhs=SPs[:, 1 - k, :], start=False,
                                                      stop=True))
                        rd.append(SPsB[1 - k])
                    P.group("pe", fns, reads=rd, writes=[psb[4], psb[5]])

                def act_w(u):
                    k = u % 2
                    ek = u % 3
                    P.op("act", lambda e: e.activation(out=XC[:, k, :, :], in_=psum[:, 4:6, :], func=AF.Exp),
                         reads=[psb[4], psb[5]], writes=[XCB[k]])
                    P.op("dve", lambda e: e.tensor_tensor(out=Wb[:, k, :, :], in0=Eb[:, ek, :, :], in1=XC[:, k, :, :],
                                                          op=ALU.mult),
                         reads=[EB[ek], XCB[k]], writes=[WB[k]])

                def wv(u):
                    un = units[u]
                    k = u % 2
                    obank = 6 + un["qi"] % 2
                    P.group("pe", [
                        lambda e: e.matmul(ps(obank), lhsT=VV[:, hk, un["ka"], :], rhs=Wb[:, k, 0, :],
                                           start=un["first"], stop=False, skip_group_check=True),
                        lambda e: e.matmul(ps(obank), lhsT=VV[:, hk, un["kb"], :], rhs=Wb[:, k, 1, :],
                                           start=False, stop=un["last"], skip_group_check=True)],
                        reads=[VVB[hk], WB[k]], writes=[psb[obank]])
                    if un["last"]:
                        yk = un["qi"] % 2
                        m = un["m"]
                        P.op("dve", lambda e: e.tensor_copy(out=yst[:, yk, :], in_=ps(obank)),
                             reads=[psb[obank]], writes=[ystB[yk]])
                        P.op("pool", lambda e: e.dma_start(out=yb_s[hd, :, m * 512:(m + 1) * 512], in_=yst[:, yk, :]),
                             reads=[ystB[yk]], writes=[], dsem=ystS[yk])

                for s in range(-3, U):
                    if 0 <= s + 3 < U:
                        qk1(s + 3)
                    if 0 <= s + 1 < U:
                        act_sp(s + 1)
                    if 0 <= s < U:
                        act_w(s)
                    if 0 <= s + 2 < U:
                        act_e(s + 2)
                    if 0 <= s + 1 < U:
                        cgroup(s + 1)
                    if 0 <= s < U:
                        wv(s)
                    if 0 <= s < U and units[s]["first"] and units[s]["m"] + 1 < NQG:
                        load_q(units[s]["m"] + 1)

            for hd in range(NH):
                do_head(hd, hd * NQG)
            P.wait_all("pool", [(s, s.count) for s in ystS])
            P.run_block()

        with ExitStack() as es2:
          if 'C' in PHASES:
            def sb2(name, shape, dt):
                return es2.enter_context(nc.sbuf_tensor("ph3_" + name, list(shape), dt))

            xs = sb2("xs", [128, 2, D], F32)
            xsB = bufs(2)
            xsS = [dsem(), dsem()]
            xoS = [dsem(), dsem()]
            xT = sb2("xT", [128, DC, 512], F32)
            xTB = bufs(DC)
            hT = sb2("hT", [128, DC, 512], BF16)
            hTB = bufs(DC)
            big = sb2("big", [128, FC, 512], BF16)
            bigB = bufs(FC)
            bigS = [dsem() for _ in range(6)]
            sgs = sb2("sgs", [128, 4, 512], F32)
            sgsB = bufs(4)
            sq = sb2("sq", [128, 2, 512], BF16)
            sqB = bufs(2)
            rsb = sb2("rsb", [128, 512], F32)
            rsbB = Buf()
            tmpf = sb2("tmpf", [128, 2, 512], F32)
            tmpfB = bufs(2)
            pin = sb2("pin", [128, 4, 256], F32)
            pinB = Buf()
            pinS = dsem()
            pbf = sb2("pbf", [128, 4, 256], BF16)
            pbfB = Buf()
            pT = sb2("pT", [128, 2, 512], BF16)
            pTB = bufs(2)
            ring = Ring(sb2, "ringC", 3)
            ring2 = sb2("ring2", [128, 2, 2, 512], BF16)
            ring2B = bufs(2)
            ring2S = [dsem(), dsem()]
            obank = [0]

            def mm_group(bank, s, j, KC, in_ap, inB, start, stop, kofs=0):
                P.group("pe", [lambda e, c=c: e.matmul(
                    ps(bank), lhsT=ring.t[:, s, c, j * 128:(j + 1) * 128], rhs=in_ap(kofs + c),
                    start=(start and c == 0), stop=(stop and c == KC - 1), skip_group_check=True) for c in range(KC)],
                    reads=list(inB) + [ring.B[s]], writes=[psb[bank]])

            def layer(key, wc, groups, KC, in_ap, inB, evac, nb=4):
                ci = 0
                for g in groups:
                    s = ring.load(key, wc, g, 0, KC)
                    for j in range(4):
                        bank = obank[0] % nb
                        obank[0] += 1
                        mm_group(bank, s, j, KC, in_ap, inB, True, True)
                        evac(ci, bank)
                        ci += 1

            def norm_fm(gi):
                for c in range(DC):
                    k = c % 2
                    P.op("act", lambda e, c=c, k=k: e.activation(out=sq[:, k, :], in_=xT[:, c, :], func=AF.Square),
                         reads=[xTB[c]], writes=[sqB[k]])
                    P.op("pe", lambda e, c=c, k=k: e.matmul(ps(4), lhsT=ones_b[:], rhs=sq[:, k, :], start=(c == 0),
                                                            stop=(c == DC - 1), skip_group_check=True),
                         reads=[sqB[k], B_const], writes=[psb[4]])
                rstd_from_ps(4, rsb[:], rsbB, D)
                for c in range(DC):
                    P.op("dve", lambda e, c=c: e.scalar_tensor_tensor(out=hT[:, c, :], in0=xT[:, c, :],
                                                                      scalar=gT[:, gi, c:c + 1], in1=rsb[:],
                                                                      op0=ALU.mult, op1=ALU.mult),
                         reads=[xTB[c], rsbB, B_const], writes=[hTB[c]])

            def c_tile(t):
                c0 = t * 512
                P.op("sp", lambda e: e.dma_start(out=big[:, 0:8, :], in_=yb_s[:, :, c0:c0 + 512].rearrange("h p t -> p h t")),
                     writes=bigB[0:8], dsem=bigS[0])
                if CSTOP < -2:
                    return
                P.op("sp", lambda e: e.dma_start(out=pin[:], in_=po[c0:c0 + 512, :].rearrange("(s p) f -> p s f", p=128)),
                     writes=[pinB], dsem=pinS)
                if CSTOP < -1:
                    return
                for sub in range(4):
                    k = sub % 2
                    P.op("sp", lambda e, sub=sub, k=k: e.dma_start(out=xs[:, k, :], in_=xo[c0 + sub * 128:c0 + (sub + 1) * 128, :]),
                         writes=[xsB[k]], dsem=xsS[k])
                    for c in range(DC):
                        bank = 4 + (c % 4)
                        P.op("pe", lambda e, c=c, k=k, bank=bank: e.transpose(
                            ps(bank, 128), xs[:, k, c * 128:(c + 1) * 128], ident_f[:]),
                            reads=[xsB[k], B_const], writes=[psb[bank]])
                        if c % 2 == 0:
                            P.op("dve", lambda e, c=c, sub=sub, bank=bank: e.tensor_copy(
                                out=xT[:, c, sub * 128:(sub + 1) * 128], in_=ps(bank, 128)),
                                reads=[psb[bank], xTB[c]], writes=[xTB[c]])
                        else:
                            P.op("act", lambda e, c=c, sub=sub, bank=bank: e.activation(
                                out=xT[:, c, sub * 128:(sub + 1) * 128], in_=ps(bank, 128),
                                func=AF.Copy), reads=[psb[bank], xTB[c]], writes=[xTB[c]])
                if CSTOP < 0:
                    return
                P.op("dve", lambda e: e.tensor_copy(out=pbf[:], in_=pin[:]), reads=[pinB], writes=[pbfB])
                for f in range(2):
                    P.group("pe", [lambda e, sub=sub, f=f: e.transpose(psbf(4 + f, 512)[:, sub * 128:(sub + 1) * 128],
                                                                       pbf[:, sub, f * 128:(f + 1) * 128], ident_b[:])
                                   for sub in range(4)], reads=[pbfB, B_const], writes=[psb[4 + f]])
                    P.op("dve", lambda e, f=f: e.tensor_copy(out=pT[:, f, :], in_=psbf(4 + f, 512)),
                         reads=[psb[4 + f]], writes=[pTB[f]])

                if CSTOP < 1:
                    return
                def ev_ub(ci, bank):
                    k = ci % 2
                    if ci % 4 == 0:
                        gi4 = (ci // 4) % 2
                        P.op("sp", lambda e: e.dma_start(out=big[:, 8 + 4 * gi4:12 + 4 * gi4, :],
                                                         in_=ga_s[ci:ci + 4, :, c0:c0 + 512].rearrange("h p t -> p h t")),
                             writes=bigB[8 + 4 * gi4:12 + 4 * gi4], dsem=bigS[1 + gi4])
                        P.op("sp", lambda e: e.dma_start(out=big[:, 16 + 4 * gi4:20 + 4 * gi4, :],
                                                         in_=sgb_s[ci:ci + 4, :, c0:c0 + 512].rearrange("h p t -> p h t")),
                             writes=bigB[16 + 4 * gi4:20 + 4 * gi4], dsem=bigS[3 + gi4])
                    ia = 8 + 4 * ((ci // 4) % 2) + ci % 4
                    ib = ia + 8
                    P.op("dve", lambda e: e.tensor_tensor(out=tmpf[:, k, :], in0=ps(bank), in1=big[:, ib, :], op=ALU.mult),
                         reads=[psb[bank], bigB[ib]], writes=[tmpfB[k]])
                    P.op("dve", lambda e: e.tensor_tensor(out=hT[:, ci, :], in0=tmpf[:, k, :], in1=big[:, ia, :],
                                                          op=ALU.add),
                         reads=[tmpfB[k], bigB[ia]], writes=[hTB[ci]])
                layer("ub", wc_ub, [0, 1, 2, 3], 8, lambda c: big[:, c, :], bigB[0:8], ev_ub)

                if CSTOP < 2:
                    return
                def ev_res(ci, bank):
                    P.op("dve", lambda e: e.tensor_tensor(out=xT[:, ci, :], in0=xT[:, ci, :], in1=ps(bank), op=ALU.add),
                         reads=[psb[bank], xTB[ci]], writes=[xTB[ci]])
                layer("o", wc_o, [0, 1, 2, 3], DC, lambda c: hT[:, c, :], hTB, ev_res)

                if CSTOP < 3:
                    return
                norm_fm(1)
                def ev_gate(ci, bank):
                    P.op("act", lambda e: e.activation(out=sgs[:, ci, :], in_=ps(bank), func=AF.Silu),
                         reads=[psb[bank]], writes=[sgsB[ci]])

                def mk_ev_up(gp):
                    def ev_up(ci, bank):
                        P.op("dve", lambda e: e.tensor_tensor(out=big[:, gp * 4 + ci, :], in0=ps(bank),
                                                              in1=sgs[:, ci, :], op=ALU.mult),
                             reads=[psb[bank], sgsB[ci]], writes=[bigB[gp * 4 + ci]])
                    return ev_up

                for gp in range(11):
                    layer("fi", wc_fi, [gp], DC, lambda c: hT[:, c, :], hTB, ev_gate)
                    layer("fi", wc_fi, [11 + gp], DC, lambda c: hT[:, c, :], hTB, mk_ev_up(gp))
                if CSTOP < 4:
                    return
                for g in range(4):
                    base = 4 * (g % 2)
                    parts = [(0, 16), (16, 32), (32, 44)]
                    for pi, (k0, k1) in enumerate(parts):
                        s = ring.load("fo", wc_fo, g, k0, k1)
                        for j in range(4):
                            mm_group(base + j, s, j, k1 - k0, lambda c: big[:, c, :], bigB[k0:k1], pi == 0,
                                     pi == len(parts) - 1, kofs=k0)
                    for j in range(4):
                        ev_res(g * 4 + j, base + j)

                if CSTOP < 5:
                    return
                norm_fm(2)
                for g in range(4):
                    s = ring.load("pg", wc_pg, g, 0, DC)
                    s2 = g % 2
                    P.op("sp", lambda e, g=g, s2=s2: e.dma_start(out=ring2[:, s2, :, :], in_=wc_pl[g, :, :, :]),
                         reads=[ready["pl"]], writes=[ring2B[s2]], dsem=ring2S[s2])
                    for j in range(4):
                        ci = g * 4 + j
                        ba = 2 * (ci % 2)
                        bb = ba + 1
                        k = ci % 2
                        mm_group(ba, s, j, DC, lambda c: hT[:, c, :], hTB, True, True)
                        P.group("pe", [lambda e, c=c, j=j, s2=s2, bb=bb: e.matmul(
                            ps(bb), lhsT=ring2[:, s2, c, j * 128:(j + 1) * 128], rhs=pT[:, c, :],
                            start=(c == 0), stop=(c == 1)) for c in range(2)],
                            reads=pTB + [ring2B[s2]], writes=[psb[bb]])
                        P.op("act", lambda e, ba=ba, k=k: e.activation(out=tmpf[:, k, :], in_=ps(ba), func=AF.Sigmoid),
                             reads=[psb[ba]], writes=[tmpfB[k]])
                        P.op("dve", lambda e, bb=bb, k=k: e.tensor_tensor(out=tmpf[:, k, :], in0=tmpf[:, k, :],
                                                                          in1=ps(bb), op=ALU.mult),
                             reads=[psb[bb], tmpfB[k]], writes=[tmpfB[k]])
                        P.op("dve", lambda e, ci=ci, k=k: e.tensor_tensor(out=xT[:, ci, :], in0=xT[:, ci, :],
                                                                          in1=tmpf[:, k, :], op=ALU.add),
                             reads=[tmpfB[k], xTB[ci]], writes=[xTB[ci]])

                if CSTOP < 6:
                    return
                for sub in range(4):
                    k = sub % 2
                    for c in range(DC):
                        bank = 4 + (c % 4)
                        P.op("pe", lambda e, c=c, sub=sub, bank=bank: e.transpose(
                            ps(bank, 128), xT[:, c, sub * 128:(sub + 1) * 128], ident_f[:]),
                            reads=[xTB[c], B_const], writes=[psb[bank]])
                        if c % 2 == 0:
                            P.op("dve", lambda e, c=c, k=k, bank=bank: e.tensor_copy(
                                out=xs[:, k, c * 128:(c + 1) * 128], in_=ps(bank, 128)),
                                reads=[psb[bank], xsB[k]], writes=[xsB[k]])
                        else:
                            P.op("act", lambda e, c=c, k=k, bank=bank: e.activation(
                                out=xs[:, k, c * 128:(c + 1) * 128], in_=ps(bank, 128), func=AF.Copy),
                                reads=[psb[bank], xsB[k]], writes=[xsB[k]])
                    r0 = c0 + sub * 128
                    P.op("pool", lambda e, k=k, r0=r0: e.dma_start(out=out[r0:r0 + 128, :], in_=xs[:, k, :]),
                         reads=[xsB[k]], writes=[], dsem=xoS[k])
            for t in range(NQG):
                c_tile(t)
            P.wait_all("pool", [(s, s.count) for s in xoS])
            P.run_block()
    return nc


_CACHE = {}


def _masks(i):
    j = np.arange(128)[:, None, None]
    r = np.arange(16)[None, :, None]
    t = np.arange(512)[None, None, :]
    ok = (r * 128 + j) < (4 * i * 128 + t)
    return np.where(ok, 0.0, NEG).astype(np.float32)


def kernel(**inputs):
    x = np.asarray(inputs["x"], dtype=np.float32)
    p = np.asarray(inputs["p"], dtype=np.float32)
    B, S, _ = x.shape
    NQG = S // 2048
    if NQG not in _CACHE:
        _CACHE[NQG] = build_program(NQG)
    nc = _CACHE[NQG]
    shared = {}
    for name in ("attn_norm_g", "w_in", "sgu_norm_g", "w_s", "b_s", "q_norm_g", "k_norm_g", "w_up_a", "w_up_b",
                 "w_o", "ffn_norm_g", "w_ffn_in", "w_ffn_out", "ple_norm_g", "w_ple_gate", "w_ple"):
        shared[name] = np.ascontiguousarray(np.asarray(inputs[name], dtype=np.float32)[0])
    in_maps = []
    idxs = []
    for core in range(8):
        b, i = divmod(core, 4)
        rows = np.concatenate([np.arange((4 * m + i) * 512, (4 * m + i + 1) * 512) for m in range(NQG)])
        idxs.append((b, rows))
        d = dict(shared)
        d["xb"] = np.ascontiguousarray(x[b])
        d["xo"] = np.ascontiguousarray(x[b][rows])
        d["po"] = np.ascontiguousarray(p[0, b][rows])
        d["maskd"] = _masks(i)
        in_maps.append(d)
    res = run_bass_kernel_spmd(nc, in_maps, core_ids=list(range(8)))
    outp = np.empty((B, S, D), dtype=np.float32)
    for core in range(8):
        b, rows = idxs[core]
        outp[b, rows] = res.results[core]["out"]
    if DEBUG:
        kernel.last = res
    return outp
```

```python
from contextlib import ExitStack

import numpy as np
import concourse.bass as bass
import concourse.mybir as mybir
from concourse.bass_utils import run_bass_kernel_spmd

F32 = mybir.dt.float32
BF16 = mybir.dt.bfloat16
AF = mybir.ActivationFunctionType
ALU = mybir.AluOpType

D = 2048
DC = 16
NH = 8
FH = 5632
FC = 44
INC = 9216
TT = 512
EPS = 1e-6
NEG = -30000.0
import os
DEBUG = bool(int(os.environ.get('KDEBUG', '0')))
PHASES = os.environ.get('KPHASES', 'KATC')
CSTOP = int(os.environ.get('KCSTOP', '99'))


class Sem:
    def __init__(self, h, step=1):
        self.h = h
        self.step = step
        self.count = 0


class Buf:
    __slots__ = ("w", "r")

    def __init__(self):
        self.w = None
        self.r = []


def bufs(n):
    return [Buf() for _ in range(n)]


class Prog:
    ENG = ("pe", "act", "dve", "pool", "sp")

    def __init__(self, nc, esem):
        self.nc = nc
        self.esem = esem
        self.q = {k: [] for k in self.ENG}
        self.waited = {k: {} for k in self.ENG}
        self.dsems = []

    def group(self, eng, fns, reads=(), writes=(), dsem=None):
        deps = {}

        def add(d):
            if d is None:
                return
            s, v = d
            if deps.get(s, 0) < v:
                deps[s] = v

        for b in reads:
            add(b.w)
        for b in writes:
            add(b.w)
            for d in b.r:
                add(d)
        waited = self.waited[eng]
        wl = []
        for s, v in deps.items():
            if eng in ("pe", "sp") and s is self.esem.get(eng):
                continue
            if waited.get(s, 0) >= v:
                continue
            waited[s] = v
            wl.append((s.h, v))
        sem = dsem if dsem is not None else self.esem[eng]
        sem.count += sem.step
        val = sem.count
        h, step = sem.h, sem.step

        def thunk(e):
            for (sh, v) in wl:
                e.wait_ge(sh, v)
            ins = None
            for f in fns:
                ins = f(e)
            ins.then_inc(h, step)

        self.q[eng].append(thunk)
        tok = (sem, val)
        for b in writes:
            b.w = tok
            b.r = []
        for b in reads:
            b.r.append(tok)
        return tok

    def op(self, eng, fn, reads=(), writes=(), dsem=None):
        return self.group(eng, [fn], reads, writes, dsem)

    def wait_all(self, eng, toks):
        best = {}
        for t in toks:
            if t is None:
                continue
            s, v = t
            if best.get(s, 0) < v:
                best[s] = v
        wl = [(s.h, v) for s, v in best.items()]

        def thunk(e):
            for (sh, v) in wl:
                e.wait_ge(sh, v)

        self.q[eng].append(thunk)

    def run_block(self):
        nc = self.nc
        q = self.q
        with nc.Block() as blk:
            @blk.tensor
            def _(e):
                for f in q["pe"]:
                    f(e)

            @blk.scalar
            def _(e):
                for f in q["act"]:
                    f(e)

            @blk.vector
            def _(e):
                for f in q["dve"]:
                    f(e)

            @blk.gpsimd
            def _(e):
                for f in q["pool"]:
                    f(e)

            @blk.sync
            def _(e):
                for f in q["sp"]:
                    f(e)
        self.q = {k: [] for k in self.ENG}


def build_program(NQG):
    S = 2048 * NQG
    NOWN = 512 * NQG
    NKT = S // TT
    NKB = S // 128
    nc = bass.Bass("TRN2", target_bir_lowering=False)

    def din(name, shape):
        return nc.dram_tensor(name, list(shape), F32, kind="ExternalInput").ap()

    xb = din("xb", [S, D])
    xo = din("xo", [NOWN, D])
    po = din("po", [NOWN, 256])
    maskd = din("maskd", [128, 16, 512])
    g_attn = din("attn_norm_g", [D])
    w_in = din("w_in", [D, INC])
    sgu_g = din("sgu_norm_g", [8, 128])
    w_s = din("w_s", [8, 128, 128])
    b_s = din("b_s", [8, 128])
    qg = din("q_norm_g", [128])
    kg = din("k_norm_g", [128])
    w_up_a = din("w_up_a", [1024, D])
    w_up_b = din("w_up_b", [1024, D])
    w_o = din("w_o", [D, D])
    g_ffn = din("ffn_norm_g", [D])
    w_fi = din("w_ffn_in", [D, 2 * FH])
    w_fo = din("w_ffn_out", [FH, D])
    g_ple = din("ple_norm_g", [D])
    w_pg = din("w_ple_gate", [D, D])
    w_ple = din("w_ple", [256, D])
    out = nc.dram_tensor("out", [NOWN, D], F32, kind="ExternalOutput").ap()

    skind = "ExternalOutput" if DEBUG else "Internal"

    def scratch(name, shape, dt=BF16):
        return nc.dram_tensor(name, list(shape), dt, kind=skind).ap()

    kT_s = scratch("kT_s", [NH, 128, S])
    v_s = scratch("v_s", [S, 1024])
    qT_s = scratch("qT_s", [NH, 128, NOWN])
    ga_s = scratch("ga_s", [DC, 128, NOWN])
    sgb_s = scratch("sgb_s", [DC, 128, NOWN])
    yb_s = scratch("yb_s", [NH, 128, NOWN])

    def wcache(name, K, N):
        return nc.dram_tensor(name, [N // 512, 128, K // 128, 512], BF16, kind="Internal").ap()

    wc_in = wcache("wc_in", D, INC)
    wc_ua = wcache("wc_ua", 1024, D)
    wc_ub = wcache("wc_ub", 1024, D)
    wc_o = wcache("wc_o", D, D)
    wc_fi = wcache("wc_fi", D, 2 * FH)
    wc_fo = wcache("wc_fo", FH, D)
    wc_pg = wcache("wc_pg", D, D)
    wc_pl = wcache("wc_pl", 256, D)

    with ExitStack() as es:
        def sb(name, shape, dt):
            return es.enter_context(nc.sbuf_tensor(name, list(shape), dt))

        def newsem(name, step=1):
            return Sem(es.enter_context(nc.semaphore(name)), step)

        esem = {k: newsem("e_" + k) for k in ("pe", "act", "dve", "pool")}
        esem["sp"] = newsem("e_sp_unused")
        P = Prog(nc, esem)
        nds = [0]

        def dsem():
            nds[0] += 1
            return newsem("d%d" % nds[0], 16)

        ident_f = sb("ident_f", [128, 128], F32)
        ident_b = sb("ident_b", [128, 128], BF16)
        ones_b = sb("ones_b", [128, 128], BF16)
        negones = sb("negones", [128, 128], BF16)
        negtri = sb("negtri", [128, 128], BF16)
        gT = sb("gT", [128, 3, DC], F32)
        sgT = sb("sgT", [128, 8], F32)
        gq = sb("gq", [128, 2], F32)
        wsT = sb("wsT", [128, 8, 128], BF16)
        bsb = sb("bsb", [128, 1024], F32)
        wtmp = sb("wtmp", [128, 128], F32)
        wtmp2 = sb("wtmp2", [128, 128], F32)
        psum = es.enter_context(nc.psum_tensor("psum", [128, 8, 512], F32))
        psb = [Buf() for _ in range(8)]

        def ps(i, n=512):
            return psum[:, i, 0:n]

        def psbf(i, n):
            return psum[:, i, :].bitcast(BF16)[:, 0:n]

        B_const = Buf()
        gstage = sb("gstage", [16, 4, 128], F32)
        B_gst = bufs(4)
        d_gst = [dsem() for _ in range(4)]
        B_wtmp = Buf()
        B_wtmp2 = Buf()
        d_const = dsem()
        d_wtmp = dsem()

        ready = {}

        cast_jobs = []
        cast_sems = {}

        def cast_weight(key, wc, w, K, N, groups=None, after=()):
            KC = K // 128
            G = N // 512
            for g in (groups if groups is not None else range(G)):
                for k0 in range(0, KC, 16):
                    k1 = min(KC, k0 + 16)
                    src = w[k0 * 128:k1 * 128, g * 512:(g + 1) * 512].rearrange("(kc p) j -> p kc j", p=128)
                    dst = wc[g, :, k0:k1, :]
                    cast_jobs.append((key, src, dst))

        def emit_casts(n):
            for _ in range(min(n, len(cast_jobs))):
                key, src, dst = cast_jobs.pop(0)
                if key not in cast_sems:
                    cast_sems[key] = dsem()
                    ready[key] = Buf()
                tok = P.op("pool", lambda e, s=src, d=dst: e.dma_start(out=d, in_=s), dsem=cast_sems[key])
                ready[key].w = tok

        def setup_consts():
            P.op("pool", lambda e: e.memset(ident_f[:], 1.0), writes=[B_const])
            P.op("pool", lambda e: e.affine_select(out=ident_f[:], in_=ident_f[:], pattern=[[1, 128]],
                                                   compare_op=ALU.is_equal, fill=0.0, base=0,
                                                   channel_multiplier=-1), writes=[B_const])
            P.op("pool", lambda e: e.tensor_copy(out=ident_b[:], in_=ident_f[:]), reads=[B_const], writes=[B_const])
            P.op("pool", lambda e: e.memset(ones_b[:], 1.0), writes=[B_const])
            P.op("pool", lambda e: e.memset(negones[:], -1.0), writes=[B_const])
            P.op("pool", lambda e: e.memset(negtri[:], -1.0), writes=[B_const])
            P.op("pool", lambda e: e.affine_select(out=negtri[:], in_=negtri[:], pattern=[[-1, 128]],
                                                   compare_op=ALU.is_ge, fill=0.0, base=0,
                                                   channel_multiplier=1), writes=[B_const])
            for i, g in enumerate((g_attn, g_ffn, g_ple, sgu_g)):
                rows = 8 if i == 3 else DC
                src = g if i == 3 else g.rearrange("(c p) -> c p", p=128)
                P.op("sp", lambda e, i=i, src=src, rows=rows: e.dma_start(out=gstage[0:rows, i, :], in_=src),
                     writes=[B_gst[i]], dsem=d_gst[i])
                P.op("pe", lambda e, i=i, rows=rows: e.transpose(ps(7, rows), gstage[0:rows, i, :],
                                                                 ident_f[0:rows, 0:rows]),
                     reads=[B_gst[i], B_const], writes=[psb[7]])
                dst = sgT[:] if i == 3 else gT[:, i, :]
                P.op("dve", lambda e, dst=dst, rows=rows: e.tensor_copy(out=dst, in_=ps(7, rows)),
                     reads=[psb[7]], writes=[B_const])
            P.op("sp", lambda e: e.dma_start(out=gq[:, 0:1], in_=qg.rearrange("(p o) -> p o", o=1)),
                 writes=[B_const], dsem=d_const)
            P.op("sp", lambda e: e.dma_start(out=gq[:, 1:2], in_=kg.rearrange("(p o) -> p o", o=1)),
                 writes=[B_const], dsem=d_const)
            P.op("sp", lambda e: e.dma_start(out=bsb[:], in_=b_s.rearrange("g t -> (g t)").partition_broadcast(128)),
                 writes=[B_const], dsem=d_const)
            P.op("dve", lambda e: e.scalar_tensor_tensor(out=gq[:, 0:1], in0=gq[:, 0:1], scalar=float(128 ** -0.5),
                                                         in1=gq[:, 1:2], op0=ALU.mult, op1=ALU.mult),
                 reads=[B_const], writes=[B_const])
            for g in range(8):
                P.op("sp", lambda e, g=g: e.dma_start(out=wtmp[:], in_=w_s[g]), writes=[B_wtmp], dsem=d_wtmp)
                P.op("pe", lambda e: e.transpose(ps(7, 128), wtmp[:], ident_f[:]), reads=[B_wtmp, B_const],
                     writes=[psb[7]])
                P.op("dve", lambda e: e.tensor_copy(out=wtmp2[:], in_=ps(7, 128)), reads=[psb[7]], writes=[B_wtmp2])
                P.op("pool", lambda e, g=g: e.affine_select(out=wsT[:, g, :], in_=wtmp2[:], pattern=[[1, 128]],
                                                            compare_op=ALU.is_ge, fill=0.0, base=0,
                                                            channel_multiplier=-1),
                     reads=[B_wtmp2], writes=[B_const])

        def rstd_from_ps(bank, dst, dstB, nfeat):
            P.op("act", lambda e: e.activation(out=dst, in_=ps(bank), func=AF.Ln, bias=EPS, scale=1.0 / nfeat),
                 reads=[psb[bank]], writes=[dstB])
            P.op("act", lambda e: e.activation(out=dst, in_=dst, func=AF.Exp, scale=-0.5),
                 reads=[dstB], writes=[dstB])

        class Ring:
            def __init__(self, alloc, name, nslots):
                self.t = alloc(name, [128, nslots, 16, 512], BF16)
                self.B = bufs(nslots)
                self.sem = [dsem() for _ in range(nslots)]
                self.n = nslots
                self.i = 0

            def load(self, key, wc, g, k0, k1):
                s = self.i % self.n
                self.i += 1
                cb = ready[key]
                P.op("sp", lambda e, s=s: e.dma_start(out=self.t[:, s, 0:k1 - k0, :], in_=wc[g, :, k0:k1, :]),
                     reads=[cb], writes=[self.B[s]], dsem=self.sem[s])
                return s

        def norm_tm(src_rows, gi, xs, xsB, xsS, xn, xnB, junk, junkB, ss, ssB, hT, hTB, trbank, part="both"):
            if part in ("both", "pre"):
                norm_tm_pre(src_rows, xs, xsB, xsS, xn, xnB, junk, junkB, ss, ssB)
            if part in ("both", "post"):
                norm_tm_post(gi, xn, xnB, hT, hTB, trbank)

        def norm_tm_pre(src_rows, xs, xsB, xsS, xn, xnB, junk, junkB, ss, ssB):
            P.op("dve", lambda e: e.memset(ss[:], 0.0), writes=[ssB])
            for sub in range(4):
                k = sub % 2
                P.op("sp", lambda e, sub=sub, k=k: e.dma_start(out=xs[:, k, :], in_=src_rows[sub * 128:(sub + 1) * 128, :]),
                     writes=[xsB[k]], dsem=xsS[k])
                P.op("act", lambda e, sub=sub, k=k: e.activation(out=junk[:], in_=xs[:, k, :], func=AF.Square,
                                                                 accum_out=ss[:, sub:sub + 1]),
                     reads=[xsB[k]], writes=[junkB, ssB])
                P.op("act", lambda e, sub=sub: e.activation(out=ss[:, 4 + sub:5 + sub], in_=ss[:, sub:sub + 1],
                                                            func=AF.Ln, bias=EPS, scale=1.0 / D),
                     reads=[ssB], writes=[ssB])
                P.op("act", lambda e, sub=sub: e.activation(out=ss[:, 4 + sub:5 + sub], in_=ss[:, 4 + sub:5 + sub],
                                                            func=AF.Exp, scale=-0.5),
                     reads=[ssB], writes=[ssB])
                P.op("dve", lambda e, sub=sub, k=k: e.tensor_scalar(out=xn[:, sub, :], in0=xs[:, k, :],
                                                                    scalar1=ss[:, 4 + sub:5 + sub], scalar2=None,
                                                                    op0=ALU.mult),
                     reads=[xsB[k], ssB], writes=[xnB[sub]])

        def norm_tm_post(gi, xn, xnB, hT, hTB, trbank):
            for c in range(DC):
                bank = trbank[c % len(trbank)]
                P.group("pe", [lambda e, c=c, sub=sub, bank=bank: e.transpose(
                    psbf(bank, 512)[:, sub * 128:(sub + 1) * 128], xn[:, sub, c * 128:(c + 1) * 128], ident_b[:])
                    for sub in range(4)], reads=xnB + [B_const], writes=[psb[bank]])
                if c % 2 == 0:
                    P.op("dve", lambda e, c=c, bank=bank: e.tensor_scalar(out=hT[:, c, :], in0=psbf(bank, 512),
                                                                          scalar1=gT[:, gi, c:c + 1], scalar2=None,
                                                                          op0=ALU.mult),
                         reads=[psb[bank], B_const], writes=[hTB[c]])
                else:
                    P.op("act", lambda e, c=c, bank=bank: e.activation(out=hT[:, c, :], in_=psbf(bank, 512),
                                                                       func=AF.Copy, scale=gT[:, gi, c:c + 1]),
                         reads=[psb[bank], B_const], writes=[hTB[c]])

        with ExitStack() as es2:
          if 'K' in PHASES:
            def sb2(name, shape, dt):
                return es2.enter_context(nc.sbuf_tensor("ph0_" + name, list(shape), dt))

            wkv = sb2("wkv", [128, 4, 16, 512], BF16)
            wkvB = bufs(4)
            xs = sb2("xs", [128, 2, D], F32)
            xsB = bufs(2)
            xsS = [dsem(), dsem()]
            xn = sb2("xn", [128, 4, D], BF16)
            xnB = bufs(4)
            junk = sb2("junk", [128, D], BF16)
            junkB = Buf()
            ss = sb2("ss", [128, 8], F32)
            ssB = Buf()
            hT2 = sb2("hT", [128, 2, DC, 512], BF16)
            hTB2 = [bufs(DC), bufs(DC)]
            sqk = sb2("sqk", [128, 2, 512], BF16)
            sqkB = bufs(2)
            rsb = sb2("rsb", [128, 2, 512], F32)
            rsbB = bufs(2)
            kst = sb2("kst", [128, 2, 512], BF16)
            kstB = bufs(2)
            kstS = [dsem(), dsem()]
            vst = sb2("vst", [128, 2, 1024], BF16)
            vstB = bufs(2)
            vstS = [dsem(), dsem()]
            d_wkv = [dsem() for _ in range(4)]

            wstage = sb2("wstage", [128, 16, 512], F32)
            wstageB = Buf()
            d_wst = dsem()
            for i, g in enumerate((6, 7, 8, 9)):
                P.op("sp", lambda e, g=g: e.dma_start(
                    out=wstage[:], in_=w_in[:, g * 512:(g + 1) * 512].rearrange("(kc p) j -> p kc j", p=128)),
                    writes=[wstageB], dsem=d_wst)
                P.op("dve", lambda e, i=i: e.tensor_copy(out=wkv[:, i, 0:8, :], in_=wstage[:, 0:8, :]),
                     reads=[wstageB], writes=[wkvB[i]])
                P.op("act", lambda e, i=i: e.activation(out=wkv[:, i, 8:16, :], in_=wstage[:, 8:16, :], func=AF.Copy),
                     reads=[wstageB, wkvB[i]], writes=[wkvB[i]])
            setup_consts()
            cast_weight("in", wc_in, w_in, D, INC, groups=[0, 1, 2, 3, 4, 5] + list(range(10, 18)), after=wkvB)
            cast_weight("ua", wc_ua, w_up_a, 1024, D)
            cast_weight("ub", wc_ub, w_up_b, 1024, D)
            cast_weight("o", wc_o, w_o, D, D)
            cast_weight("fi", wc_fi, w_fi, D, 2 * FH)
            cast_weight("fo", wc_fo, w_fo, FH, D)
            cast_weight("pg", wc_pg, w_pg, D, D)
            cast_weight("pl", wc_pl, w_ple, 256, D)
            ukv = [0, 0]

            def kv_norm(t, part):
                norm_tm(xb[t * 512:(t + 1) * 512, :], 0, xs, xsB, xsS, xn, xnB, junk, junkB, ss, ssB,
                        hT2[:, t % 2], hTB2[t % 2], [6, 7], part=part)

            def kv_tile(t):
                uk, uv = ukv
                hT = hT2[:, t % 2]
                hTB = hTB2[t % 2]
                if t == 0:
                    kv_norm(0, "both")
                def k_main(hd, k):
                    bank = (uk + hd) % 3
                    wi, wo = divmod(hd * 128, 512)
                    P.group("pe", [lambda e, c=c: e.matmul(
                        ps(bank), lhsT=wkv[:, wi, c, wo:wo + 128], rhs=hT[:, c, :], start=(c == 0), stop=(c == DC - 1))
                        for c in range(DC)], reads=hTB + [wkvB[wi]], writes=[psb[bank]])
                    P.op("act", lambda e: e.activation(out=sqk[:, k, :], in_=ps(bank), func=AF.Square),
                         reads=[psb[bank]], writes=[sqkB[k]])

                def k_tail(hd, k):
                    bank = (uk + hd) % 3
                    sbank = 3
                    P.op("pe", lambda e: e.matmul(ps(sbank), lhsT=ones_b[:], rhs=sqk[:, k, :], start=True, stop=True),
                         reads=[sqkB[k], B_const], writes=[psb[sbank]])
                    rstd_from_ps(sbank, rsb[:, k, :], rsbB[k], 128)
                    P.op("dve", lambda e: e.tensor_tensor(out=kst[:, k, :], in0=ps(bank), in1=rsb[:, k, :], op=ALU.mult),
                         reads=[psb[bank], rsbB[k]], writes=[kstB[k]])
                    P.op("pool", lambda e: e.dma_start(out=kT_s[hd, :, t * 512:(t + 1) * 512], in_=kst[:, k, :]),
                         reads=[kstB[k]], writes=[], dsem=kstS[k])

                for hd in range(NH + 1):
                    if hd == 2 and t + 1 < NKT:
                        kv_norm(t + 1, "pre")
                    if hd < NH:
                        k_main(hd, (uk + hd) % 2)
                    if hd >= 1:
                        k_tail(hd - 1, (uk + hd - 1) % 2)
                uk += NH
                if t + 1 < NKT:
                    kv_norm(t + 1, "post")
                for sub in range(4):
                    k = uv % 2
                    uv += 1
                    for cg in range(2):
                        bank = 4 + cg
                        P.group("pe", [lambda e, c=c, sub=sub, cg=cg, bank=bank: e.matmul(
                            ps(bank), lhsT=hT[:, c, sub * 128:(sub + 1) * 128], rhs=wkv[:, 2 + cg, c, :],
                            start=(c == 0), stop=(c == DC - 1)) for c in range(DC)],
                            reads=hTB + [wkvB[2 + cg]], writes=[psb[bank]])
                        if cg == 0:
                            P.op("dve", lambda e, k=k, bank=bank: e.tensor_copy(out=vst[:, k, 0:512], in_=ps(bank)),
                                 reads=[psb[bank]], writes=[vstB[k]])
                        else:
                            P.op("act", lambda e, k=k, bank=bank: e.activation(out=vst[:, k, 512:1024], in_=ps(bank),
                                                                               func=AF.Copy),
                                 reads=[psb[bank], vstB[k]], writes=[vstB[k]])
                    r0 = t * 512 + sub * 128
                    P.op("pool", lambda e, k=k, r0=r0: e.dma_start(out=v_s[r0:r0 + 128, :], in_=vst[:, k, :]),
                         reads=[vstB[k]], writes=[], dsem=vstS[k])
                ukv[0], ukv[1] = uk, uv

            n_early = 18
            per_tile = (n_early + NKT - 2) // max(1, NKT - 1)
            left = n_early
            for t in range(NKT):
                kv_tile(t)
                n = min(per_tile, left)
                emit_casts(n)
                left -= n
            emit_casts(left)
            P.wait_all("pool", [(s, s.count) for s in kstS + vstS])
            P.run_block()

        with ExitStack() as es2:
          if 'A' in PHASES:
            def sb2(name, shape, dt):
                return es2.enter_context(nc.sbuf_tensor("ph1_" + name, list(shape), dt))

            ring = None
            xs = sb2("xs", [128, 2, D], F32)
            xsB = bufs(2)
            xsS = [dsem(), dsem()]
            xn = sb2("xn", [128, 4, D], BF16)
            xnB = bufs(4)
            junk = sb2("junk", [128, D], BF16)
            junkB = Buf()
            ss = sb2("ss", [128, 8], F32)
            ssB = Buf()
            hT2 = sb2("hT", [128, 2, DC, 512], BF16)
            hTB2 = [bufs(DC), bufs(DC)]
            u_sb = sb2("u_sb", [128, 8, 512], BF16)
            uB = bufs(8)
            ya = sb2("ya", [128, 8, 512], BF16)
            yaB = bufs(8)
            sga = sb2("sga", [128, DC, 512], BF16)
            sgaB = bufs(DC)
            gv = sb2("gv", [128, 4, 512], F32)
            gvB = bufs(4)
            sq = sb2("sq", [128, 4, 512], BF16)
            sqB = bufs(4)
            rsb = sb2("rsb", [128, 4, 512], F32)
            rsbB = bufs(4)
            vn = sb2("vn", [128, 2, 512], BF16)
            vnB = bufs(2)
            vtok = sb2("vtok", [128, 2, 512], BF16)
            vtokB = bufs(2)
            tmpf = sb2("tmpf", [128, 2, 512], F32)
            tmpfB = bufs(2)
            stg = sb2("stg", [128, 4, 512], BF16)
            stgB = bufs(4)
            stgS = [dsem() for _ in range(4)]
            ring = Ring(sb2, "ringA", 3)
            stg_i = [0]

            def store_stage(dst_ap_fn, producer):
                k = stg_i[0] % 4
                stg_i[0] += 1
                producer(k)
                P.op("pool", lambda e, k=k: e.dma_start(out=dst_ap_fn(), in_=stg[:, k, :]),
                     reads=[stgB[k]], writes=[], dsem=stgS[k])

            obank = [0]

            pending = []

            def run_pending():
                for st in list(pending):
                    st.pop(0)()
                    if not st:
                        pending.remove(st)

            def flush_pending():
                while pending:
                    run_pending()

            def layer(key, wc, groups, KC, in_ap, inB, evac):
                ci = 0
                for g in groups:
                    s = ring.load(key, wc, g, 0, KC)
                    for j in range(4):
                        bank = obank[0] % 4
                        obank[0] += 1
                        P.group("pe", [lambda e, c=c, s=s, j=j, bank=bank: e.matmul(
                            ps(bank), lhsT=ring.t[:, s, c, j * 128:(j + 1) * 128], rhs=in_ap(c),
                            start=(c == 0), stop=(c == KC - 1)) for c in range(KC)],
                            reads=list(inB) + [ring.B[s]], writes=[psb[bank]])
                        run_pending()
                        st = evac(ci, bank)
                        if st:
                            pending.append(list(st))
                        ci += 1

            def a_norm(t, part):
                norm_tm(xo[t * 512:(t + 1) * 512, :], 0, xs, xsB, xsS, xn, xnB, junk, junkB, ss, ssB,
                        hT2[:, t % 2], hTB2[t % 2], [6, 7], part=part)

            def a_tile(t):
                c0 = t * 512
                hT = hT2[:, t % 2]
                hTB = hTB2[t % 2]
                if t == 0:
                    a_norm(0, "both")

                def ev_u(ci, bank):
                    P.op("act", lambda e: e.activation(out=u_sb[:, ci, :], in_=ps(bank), func=AF.Gelu),
                         reads=[psb[bank]], writes=[uB[ci]])
                layer("in", wc_in, [0, 1], DC, lambda c: hT[:, c, :], hTB, ev_u)

                def ev_v(ci, bank):
                    k = ci % 4
                    k2 = ci % 2
                    P.op("act", lambda e: e.activation(out=gv[:, k, :], in_=ps(bank), func=AF.Gelu),
                         reads=[psb[bank]], writes=[gvB[k]])
                    P.op("act", lambda e: e.activation(out=sq[:, k, :], in_=gv[:, k, :], func=AF.Square),
                         reads=[gvB[k]], writes=[sqB[k]])

                    def s1():
                        P.op("pe", lambda e: e.matmul(ps(4), lhsT=ones_b[:], rhs=sq[:, k, :], start=True, stop=True),
                             reads=[sqB[k], B_const], writes=[psb[4]])
                        rstd_from_ps(4, rsb[:, k, :], rsbB[k], 128)
                        P.op("dve", lambda e: e.scalar_tensor_tensor(out=vn[:, k2, :], in0=gv[:, k, :],
                                                                     scalar=sgT[:, ci:ci + 1], in1=rsb[:, k, :],
                                                                     op0=ALU.mult, op1=ALU.mult),
                             reads=[gvB[k], rsbB[k], B_const], writes=[vnB[k2]])

                    def s2():
                        P.group("pe", [lambda e, sub=sub: e.transpose(psbf(5, 512)[:, sub * 128:(sub + 1) * 128],
                                                                      vn[:, k2, sub * 128:(sub + 1) * 128], ident_b[:])
                                       for sub in range(4)], reads=[vnB[k2], B_const], writes=[psb[5]])
                        P.op("dve", lambda e: e.tensor_copy(out=vtok[:, k2, :], in_=psbf(5, 512)),
                             reads=[psb[5]], writes=[vtokB[k2]])

                    def s3():
                        P.group("pe", [lambda e, sub=sub: e.matmul(ps(6)[:, sub * 128:(sub + 1) * 128],
                                                                   lhsT=vtok[:, k2, sub * 128:(sub + 1) * 128],
                                                                   rhs=wsT[:, ci, :], start=True, stop=True)
                                       for sub in range(4)], reads=[vtokB[k2], B_const], writes=[psb[6]])
                        P.group("dve", [lambda e, sub=sub: e.tensor_tensor(
                            out=tmpf[:, k2, sub * 128:(sub + 1) * 128], in0=ps(6)[:, sub * 128:(sub + 1) * 128],
                            in1=bsb[:, ci * 128:(ci + 1) * 128], op=ALU.add) for sub in range(4)],
                            reads=[psb[6], B_const], writes=[tmpfB[k2]])
                        P.op("dve", lambda e: e.tensor_tensor(out=ya[:, ci, :], in0=tmpf[:, k2, :], in1=u_sb[:, ci, :],
                                                              op=ALU.mult),
                             reads=[tmpfB[k2], uB[ci]], writes=[yaB[ci]])
                    return [s1, s2, s3]
                layer("in", wc_in, [2, 3], DC, lambda c: hT[:, c, :], hTB, ev_v)

                def ev_q(ci, bank):
                    k = ci % 4
                    P.op("act", lambda e: e.activation(out=sq[:, k, :], in_=ps(bank), func=AF.Square),
                         reads=[psb[bank]], writes=[sqB[k]])

                    def s1():
                        P.op("pe", lambda e: e.matmul(ps(4), lhsT=ones_b[:], rhs=sq[:, k, :], start=True, stop=True),
                             reads=[sqB[k], B_const], writes=[psb[4]])
                        rstd_from_ps(4, rsb[:, k, :], rsbB[k], 128)
                        store_stage(lambda: qT_s[ci, :, c0:c0 + 512],
                                    lambda kk: P.op("dve", lambda e: e.scalar_tensor_tensor(
                                        out=stg[:, kk, :], in0=ps(bank), scalar=gq[:, 0:1], in1=rsb[:, k, :],
                                        op0=ALU.mult, op1=ALU.mult), reads=[psb[bank], rsbB[k], B_const],
                                        writes=[stgB[kk]]))
                    return [s1]
                layer("in", wc_in, [4, 5], DC, lambda c: hT[:, c, :], hTB, ev_q)

                if t + 1 < NQG:
                    a_norm(t + 1, "pre")
                def ev_ga(ci, bank):
                    P.op("act", lambda e: e.activation(out=sga[:, ci, :], in_=ps(bank), func=AF.Sigmoid),
                         reads=[psb[bank]], writes=[sgaB[ci]])
                layer("in", wc_in, [10, 11, 12, 13], DC, lambda c: hT[:, c, :], hTB, ev_ga)

                if t + 1 < NQG:
                    a_norm(t + 1, "post")
                def ev_gb(ci, bank):
                    store_stage(lambda: sgb_s[ci, :, c0:c0 + 512],
                                lambda kk: P.op("act", lambda e: e.activation(out=stg[:, kk, :], in_=ps(bank),
                                                                               func=AF.Sigmoid),
                                                reads=[psb[bank]], writes=[stgB[kk]]))
                layer("in", wc_in, [14, 15, 16, 17], DC, lambda c: hT[:, c, :], hTB, ev_gb)

                def ev_ua(ci, bank):
                    store_stage(lambda: ga_s[ci, :, c0:c0 + 512],
                                lambda kk: P.op("dve", lambda e: e.tensor_tensor(out=stg[:, kk, :], in0=ps(bank),
                                                                                 in1=sga[:, ci, :], op=ALU.mult),
                                                reads=[psb[bank], sgaB[ci]], writes=[stgB[kk]]))
                flush_pending()
                layer("ua", wc_ua, [0, 1, 2, 3], 8, lambda c: ya[:, c, :], yaB, ev_ua)
                flush_pending()

            per_tile_a = (len(cast_jobs) + NQG - 1) // NQG
            for t in range(NQG):
                a_tile(t)
                emit_casts(per_tile_a)
            emit_casts(len(cast_jobs))
            P.wait_all("pool", [(s, s.count) for s in stgS])
            P.run_block()

        with ExitStack() as es2:
          if 'T' in PHASES:
            def sb2(name, shape, dt):
                return es2.enter_context(nc.sbuf_tensor("ph2_" + name, list(shape), dt))

            masks = sb2("masks", [128, 16, 512], BF16)
            maskB = Buf()
            d_mask = dsem()
            KT = sb2("KT", [128, 2, S], BF16)
            KTB = bufs(2)
            KTS = [dsem(), dsem()]
            VV = sb2("VV", [128, 2, NKB, 128], BF16)
            VVB = bufs(2)
            VVS = [dsem(), dsem()]
            qT = sb2("qT", [128, 2, 512], BF16)
            qTB = bufs(2)
            qTS = [dsem(), dsem()]
            Eb = sb2("Eb", [128, 3, 2, 512], F32)
            EB = bufs(3)
            XC = sb2("XC", [128, 2, 2, 512], F32)
            XCB = bufs(2)
            SPs = sb2("SPs", [128, 2, 512], BF16)
            SPsB = bufs(2)
            SPt = sb2("SPt", [128, 2, 512], BF16)
            SPtB = bufs(2)
            SPb = sb2("SPb", [128, 2, 2, 512], BF16)
            SPB = bufs(2)
            Wb = sb2("Wb", [128, 2, 2, 512], BF16)
            WB = bufs(2)
            yst = sb2("yst", [128, 2, 512], BF16)
            ystB = bufs(2)
            ystS = [dsem(), dsem()]

            P.op("pool", lambda e: e.dma_start(out=masks[:], in_=maskd), writes=[maskB], dsem=d_mask)

            def load_head(hd):
                k = hd % 2
                nsplit = max(1, S // 4096)
                for i in range(nsplit):
                    a, b = i * (S // nsplit), (i + 1) * (S // nsplit)
                    P.op("sp", lambda e, a=a, b=b, k=k: e.dma_start(out=KT[:, k, a:b], in_=kT_s[hd, :, a:b]),
                         writes=[KTB[k]], dsem=KTS[k])
                vsrc = v_s.rearrange("(b p) c -> p b c", p=128)
                for b0 in range(0, NKB, 16):
                    P.op("sp", lambda e, b0=b0, k=k: e.dma_start(out=VV[:, k, b0:b0 + 16, :],
                                                                 in_=vsrc[:, b0:b0 + 16, hd * 128:(hd + 1) * 128]),
                         writes=[VVB[k]], dsem=VVS[k])

            load_head(0)
            def do_head(hd, gq_i):
                hk = hd % 2
                if hd + 1 < NH:
                    load_head(hd + 1)
                units = []
                for m in range(NQG):
                    nk = 16 * m + 16
                    for idx in range(nk // 2):
                        ka = nk - 1 - 2 * idx
                        kb_ = ka - 1
                        units.append(dict(m=m, ka=ka, kb=kb_, first=(idx == 0), last=(idx == nk // 2 - 1),
                                          ra=(ka - 16 * m) if ka >= 16 * m else None,
                                          rb=(kb_ - 16 * m) if kb_ >= 16 * m else None, qi=gq_i + m))
                U = len(units)

                def load_q(m):
                    qk = (gq_i + m) % 2
                    P.op("sp", lambda e: e.dma_start(out=qT[:, qk, :], in_=qT_s[hd, :, m * 512:(m + 1) * 512]),
                         writes=[qTB[qk]], dsem=qTS[qk])
                load_q(0)

                def qk1(u):
                    un = units[u]
                    qk = un["qi"] % 2
                    zb = 2 * (u % 2)
                    fns = []
                    rd = [KTB[hk], qTB[qk]]
                    for half, (kk, rr) in enumerate(((un["ka"], un["ra"]), (un["kb"], un["rb"]))):
                        fns.append(lambda e, kk=kk, rr=rr, half=half: e.matmul(
                            ps(zb + half), lhsT=KT[:, hk, kk * 128:(kk + 1) * 128], rhs=qT[:, qk, :],
                            start=True, stop=(rr is None)))
                        if rr is not None:
                            fns.append(lambda e, rr=rr, half=half: e.matmul(
                                ps(zb + half), lhsT=ident_b[:], rhs=masks[:, rr, :], start=False, stop=True))
                            rd += [maskB, B_const]
                    P.group("pe", fns, reads=rd, writes=[psb[zb], psb[zb + 1]])

                def act_e(u):
                    zb = 2 * (u % 2)
                    ek = u % 3
                    P.op("act", lambda e: e.activation(out=Eb[:, ek, :, :], in_=psum[:, zb:zb + 2, :], func=AF.Exp),
                         reads=[psb[zb], psb[zb + 1]], writes=[EB[ek]])

                def act_sp(u):
                    k = u % 2
                    ek = u % 3
                    un = units[u]
                    P.op("act", lambda e: e.activation(out=SPb[:, k, :, :], in_=Eb[:, ek, :, :], func=AF.Ln, bias=1.0,
                                                       scale=1.0),
                         reads=[EB[ek]], writes=[SPB[k]])
                    if not un["last"]:
                        if un["first"]:
                            P.op("dve", lambda e: e.tensor_tensor(out=SPs[:, k, :], in0=SPb[:, k, 0, :],
                                                                  in1=SPb[:, k, 1, :], op=ALU.add),
                                 reads=[SPB[k]], writes=[SPsB[k]])
                        else:
                            P.op("dve", lambda e: e.tensor_tensor(out=SPt[:, k, :], in0=SPb[:, k, 0, :],
                                                                  in1=SPb[:, k, 1, :], op=ALU.add),
                                 reads=[SPB[k]], writes=[SPtB[k]])
                            P.op("pool", lambda e: e.tensor_tensor(out=SPs[:, k, :], in0=SPs[:, 1 - k, :],
                                                                   in1=SPt[:, k, :], op=ALU.add),
                                 reads=[SPtB[k], SPsB[1 - k]], writes=[SPsB[k]])

                def cgroup(u):
                    un = units[u]
                    k = u % 2
                    first = un["first"]
                    fns = [lambda e: e.matmul(ps(4), lhsT=negtri[:], rhs=SPb[:, k, 0, :], start=True, stop=first)]
                    if not first:
                        fns.append(lambda e: e.matmul(ps(4), lhsT=negones[:], rhs=SPs[:, 1 - k, :], start=False,
                                                      stop=True))
                    fns.append(lambda e: e.matmul(ps(5), lhsT=negtri[:], rhs=SPb[:, k, 1, :], start=True, stop=False))
                    fns.append(lambda e: e.matmul(ps(5), lhsT=negones[:], rhs=SPb[:, k, 0, :], start=False,
                                                  stop=first))
                    rd = [SPB[k], B_const]
                    if not first:
                        fns.append(lambda e: e.matmul(ps(5), lhsT=negones[:], rhs=SPs[:, 1 - k, :], start=False,
                                                      stop=True))
                        rd.append(SPsB[1 - k])
                    P.group("pe", fns, reads=rd, writes=[psb[4], psb[5]])

                def act_w(u):
                    k = u % 2
                    ek = u % 3
                    P.op("act", lambda e: e.activation(out=XC[:, k, :, :], in_=psum[:, 4:6, :], func=AF.Exp),
                         reads=[psb[4], psb[5]], writes=[XCB[k]])
                    P.op("dve", lambda e: e.tensor_tensor(out=Wb[:, k, :, :], in0=Eb[:, ek, :, :], in1=XC[:, k, :, :],
                                                          op=ALU.mult),
                         reads=[EB[ek], XCB[k]], writes=[WB[k]])

                def wv(u):
                    un = units[u]
                    k = u % 2
                    obank = 6 + un["qi"] % 2
                    P.group("pe", [
                        lambda e: e.matmul(ps(obank), lhsT=VV[:, hk, un["ka"], :], rhs=Wb[:, k, 0, :],
                                           start=un["first"], stop=False, skip_group_check=True),
                        lambda e: e.matmul(ps(obank), lhsT=VV[:, hk, un["kb"], :], rhs=Wb[:, k, 1, :],
                                           start=False, stop=un["last"], skip_group_check=True)],
                        reads=[VVB[hk], WB[k]], writes=[psb[obank]])
                    if un["last"]:
                        yk = un["qi"] % 2
                        m = un["m"]
                        P.op("dve", lambda e: e.tensor_copy(out=yst[:, yk, :], in_=ps(obank)),
                             reads=[psb[obank]], writes=[ystB[yk]])
                        P.op("pool", lambda e: e.dma_start(out=yb_s[hd, :, m * 512:(m + 1) * 512], in_=yst[:, yk, :]),
                             reads=[ystB[yk]], writes=[], dsem=ystS[yk])

                for s in range(-3, U):
                    if 0 <= s + 3 < U:
                        qk1(s + 3)
                    if 0 <= s + 1 < U:
                        act_sp(s + 1)
                    if 0 <= s < U:
                        act_w(s)
                    if 0 <= s + 2 < U:
                        act_e(s + 2)
                    if 0 <= s + 1 < U:
                        cgroup(s + 1)
                    if 0 <= s < U:
                        wv(s)
                    if 0 <= s < U and units[s]["first"] and units[s]["m"] + 1 < NQG:
                        load_q(units[s]["m"] + 1)

            for hd in range(NH):
                do_head(hd, hd * NQG)
            P.wait_all("pool", [(s, s.count) for s in ystS])
            P.run_block()

        with ExitStack() as es2:
          if 'C' in PHASES:
            def sb2(name, shape, dt):
                return es2.enter_context(nc.sbuf_tensor("ph3_" + name, list(shape), dt))

            xs = sb2("xs", [128, 2, D], F32)
            xsB = bufs(2)
            xsS = [dsem(), dsem()]
            xoS = [dsem(), dsem()]
            xT = sb2("xT", [128, DC, 512], F32)
            xTB = bufs(DC)
            hT = sb2("hT", [128, DC, 512], BF16)
            hTB = bufs(DC)
            big = sb2("big", [128, FC, 512], BF16)
            bigB = bufs(FC)
            bigS = [dsem() for _ in range(6)]
            sgs = sb2("sgs", [128, 4, 512], F32)
            sgsB = bufs(4)
            sq = sb2("sq", [128, 4, 512], BF16)
            sqB = bufs(4)
            rsb = sb2("rsb", [128, 512], F32)
            rsbB = Buf()
            tmpf = sb2("tmpf", [128, 2, 512], F32)
            tmpfB = bufs(2)
            pin = sb2("pin", [128, 4, 256], F32)
            pinB = Buf()
            pinS = dsem()
            pbf = sb2("pbf", [128, 4, 256], BF16)
            pbfB = Buf()
            pT = sb2("pT", [128, 2, 512], BF16)
            pTB = bufs(2)
            ring = Ring(sb2, "ringC", 3)
            ring2 = sb2("ring2", [128, 2, 2, 512], BF16)
            ring2B = bufs(2)
            ring2S = [dsem(), dsem()]
            obank = [0]

            def mm_group(bank, s, j, KC, in_ap, inB, start, stop, kofs=0):
                P.group("pe", [lambda e, c=c: e.matmul(
                    ps(bank), lhsT=ring.t[:, s, c, j * 128:(j + 1) * 128], rhs=in_ap(kofs + c),
                    start=(start and c == 0), stop=(stop and c == KC - 1), skip_group_check=True) for c in range(KC)],
                    reads=list(inB) + [ring.B[s]], writes=[psb[bank]])

            def layer(key, wc, groups, KC, in_ap, inB, evac, nb=4):
                ci = 0
                for g in groups:
                    s = ring.load(key, wc, g, 0, KC)
                    for j in range(4):
                        bank = obank[0] % nb
                        obank[0] += 1
                        mm_group(bank, s, j, KC, in_ap, inB, True, True)
                        evac(ci, bank)
                        ci += 1

            def norm_fm(gi):
                def sqr(c):
                    k = c % 4
                    P.op("act", lambda e: e.activation(out=sq[:, k, :], in_=xT[:, c, :], func=AF.Square),
                         reads=[xTB[c]], writes=[sqB[k]])

                def mm(c):
                    k = c % 4
                    P.op("pe", lambda e: e.matmul(ps(4), lhsT=ones_b[:], rhs=sq[:, k, :], start=(c == 0),
                                                  stop=(c == DC - 1), skip_group_check=True),
                         reads=[sqB[k], B_const], writes=[psb[4]])
                sqr(0)
                sqr(1)
                for c in range(DC):
                    if c + 2 < DC:
                        sqr(c + 2)
                    mm(c)
                rstd_from_ps(4, rsb[:], rsbB, D)
                for c in range(DC):
                    P.op("dve", lambda e, c=c: e.scalar_tensor_tensor(out=hT[:, c, :], in0=xT[:, c, :],
                                                                      scalar=gT[:, gi, c:c + 1], in1=rsb[:],
                                                                      op0=ALU.mult, op1=ALU.mult),
                         reads=[xTB[c], rsbB, B_const], writes=[hTB[c]])

            def c_tile(t):
                c0 = t * 512
                P.op("sp", lambda e: e.dma_start(out=big[:, 0:8, :], in_=yb_s[:, :, c0:c0 + 512].rearrange("h p t -> p h t")),
                     writes=bigB[0:8], dsem=bigS[0])
                if CSTOP < -2:
                    return
                P.op("sp", lambda e: e.dma_start(out=pin[:], in_=po[c0:c0 + 512, :].rearrange("(s p) f -> p s f", p=128)),
                     writes=[pinB], dsem=pinS)
                if CSTOP < -1:
                    return
                for sub in range(4):
                    k = sub % 2
                    P.op("sp", lambda e, sub=sub, k=k: e.dma_start(out=xs[:, k, :], in_=xo[c0 + sub * 128:c0 + (sub + 1) * 128, :]),
                         writes=[xsB[k]], dsem=xsS[k])
                    for c in range(DC):
                        bank = 4 + (c % 4)
                        P.op("pe", lambda e, c=c, k=k, bank=bank: e.transpose(
                            ps(bank, 128), xs[:, k, c * 128:(c + 1) * 128], ident_f[:]),
                            reads=[xsB[k], B_const], writes=[psb[bank]])
                        if c % 2 == 0:
                            P.op("dve", lambda e, c=c, sub=sub, bank=bank: e.tensor_copy(
                                out=xT[:, c, sub * 128:(sub + 1) * 128], in_=ps(bank, 128)),
                                reads=[psb[bank], xTB[c]], writes=[xTB[c]])
                        else:
                            P.op("act", lambda e, c=c, sub=sub, bank=bank: e.activation(
                                out=xT[:, c, sub * 128:(sub + 1) * 128], in_=ps(bank, 128),
                                func=AF.Copy), reads=[psb[bank], xTB[c]], writes=[xTB[c]])
                if CSTOP < 0:
                    return
                P.op("dve", lambda e: e.tensor_copy(out=pbf[:], in_=pin[:]), reads=[pinB], writes=[pbfB])
                for f in range(2):
                    P.group("pe", [lambda e, sub=sub, f=f: e.transpose(psbf(4 + f, 512)[:, sub * 128:(sub + 1) * 128],
                                                                       pbf[:, sub, f * 128:(f + 1) * 128], ident_b[:])
                                   for sub in range(4)], reads=[pbfB, B_const], writes=[psb[4 + f]])
                    P.op("dve", lambda e, f=f: e.tensor_copy(out=pT[:, f, :], in_=psbf(4 + f, 512)),
                         reads=[psb[4 + f]], writes=[pTB[f]])

                if CSTOP < 1:
                    return
                def ev_ub(ci, bank):
                    k = ci % 2
                    if ci % 4 == 0:
                        gi4 = (ci // 4) % 2
                        P.op("sp", lambda e: e.dma_start(out=big[:, 8 + 4 * gi4:12 + 4 * gi4, :],
                                                         in_=ga_s[ci:ci + 4, :, c0:c0 + 512].rearrange("h p t -> p h t")),
                             writes=bigB[8 + 4 * gi4:12 + 4 * gi4], dsem=bigS[1 + gi4])
                        P.op("sp", lambda e: e.dma_start(out=big[:, 16 + 4 * gi4:20 + 4 * gi4, :],
                                                         in_=sgb_s[ci:ci + 4, :, c0:c0 + 512].rearrange("h p t -> p h t")),
                             writes=bigB[16 + 4 * gi4:20 + 4 * gi4], dsem=bigS[3 + gi4])
                    ia = 8 + 4 * ((ci // 4) % 2) + ci % 4
                    ib = ia + 8
                    P.op("dve", lambda e: e.tensor_tensor(out=tmpf[:, k, :], in0=ps(bank), in1=big[:, ib, :], op=ALU.mult),
                         reads=[psb[bank], bigB[ib]], writes=[tmpfB[k]])
                    P.op("dve", lambda e: e.tensor_tensor(out=hT[:, ci, :], in0=tmpf[:, k, :], in1=big[:, ia, :],
                                                          op=ALU.add),
                         reads=[tmpfB[k], bigB[ia]], writes=[hTB[ci]])
                layer("ub", wc_ub, [0, 1, 2, 3], 8, lambda c: big[:, c, :], bigB[0:8], ev_ub)

                if CSTOP < 2:
                    return
                def ev_res(ci, bank):
                    P.op("dve", lambda e: e.tensor_tensor(out=xT[:, ci, :], in0=xT[:, ci, :], in1=ps(bank), op=ALU.add),
                         reads=[psb[bank], xTB[ci]], writes=[xTB[ci]])
                layer("o", wc_o, [0, 1, 2, 3], DC, lambda c: hT[:, c, :], hTB, ev_res)

                if CSTOP < 3:
                    return
                norm_fm(1)
                def ev_gate(ci, bank):
                    P.op("act", lambda e: e.activation(out=sgs[:, ci, :], in_=ps(bank), func=AF.Silu),
                         reads=[psb[bank]], writes=[sgsB[ci]])

                def mk_ev_up(gp):
                    def ev_up(ci, bank):
                        P.op("dve", lambda e: e.tensor_tensor(out=big[:, gp * 4 + ci, :], in0=ps(bank),
                                                              in1=sgs[:, ci, :], op=ALU.mult),
                             reads=[psb[bank], sgsB[ci]], writes=[bigB[gp * 4 + ci]])
                    return ev_up

                for gp in range(11):
                    layer("fi", wc_fi, [gp], DC, lambda c: hT[:, c, :], hTB, ev_gate)
                    layer("fi", wc_fi, [11 + gp], DC, lambda c: hT[:, c, :], hTB, mk_ev_up(gp))
                if CSTOP < 4:
                    return
                for g in range(4):
                    base = 4 * (g % 2)
                    parts = [(0, 16), (16, 32), (32, 44)]
                    for pi, (k0, k1) in enumerate(parts):
                        s = ring.load("fo", wc_fo, g, k0, k1)
                        for j in range(4):
                            mm_group(base + j, s, j, k1 - k0, lambda c: big[:, c, :], bigB[k0:k1], pi == 0,
                                     pi == len(parts) - 1, kofs=k0)
                    for j in range(4):
                        ev_res(g * 4 + j, base + j)

                if CSTOP < 5:
                    return
                norm_fm(2)
                for g in range(4):
                    s = ring.load("pg", wc_pg, g, 0, DC)
                    s2 = g % 2
                    P.op("sp", lambda e, g=g, s2=s2: e.dma_start(out=ring2[:, s2, :, :], in_=wc_pl[g, :, :, :]),
                         reads=[ready["pl"]], writes=[ring2B[s2]], dsem=ring2S[s2])
                    for j in range(4):
                        ci = g * 4 + j
                        ba = 2 * (ci % 2)
                        bb = ba + 1
                        k = ci % 2
                        mm_group(ba, s, j, DC, lambda c: hT[:, c, :], hTB, True, True)
                        P.group("pe", [lambda e, c=c, j=j, s2=s2, bb=bb: e.matmul(
                            ps(bb), lhsT=ring2[:, s2, c, j * 128:(j + 1) * 128], rhs=pT[:, c, :],
                            start=(c == 0), stop=(c == 1)) for c in range(2)],
                            reads=pTB + [ring2B[s2]], writes=[psb[bb]])
                        P.op("act", lambda e, ba=ba, k=k: e.activation(out=tmpf[:, k, :], in_=ps(ba), func=AF.Sigmoid),
                             reads=[psb[ba]], writes=[tmpfB[k]])
                        P.op("dve", lambda e, bb=bb, k=k: e.tensor_tensor(out=tmpf[:, k, :], in0=tmpf[:, k, :],
                                                                          in1=ps(bb), op=ALU.mult),
                             reads=[psb[bb], tmpfB[k]], writes=[tmpfB[k]])
                        P.op("dve", lambda e, ci=ci, k=k: e.tensor_tensor(out=xT[:, ci, :], in0=xT[:, ci, :],
                                                                          in1=tmpf[:, k, :], op=ALU.add),
                             reads=[tmpfB[k], xTB[ci]], writes=[xTB[ci]])

                if CSTOP < 6:
                    return
                for sub in range(4):
                    k = sub % 2
                    for c in range(DC):
                        bank = 4 + (c % 4)
                        P.op("pe", lambda e, c=c, sub=sub, bank=bank: e.transpose(
                            ps(bank, 128), xT[:, c, sub * 128:(sub + 1) * 128], ident_f[:]),
                            reads=[xTB[c], B_const], writes=[psb[bank]])
                        if c % 2 == 0:
                            P.op("dve", lambda e, c=c, k=k, bank=bank: e.tensor_copy(
                                out=xs[:, k, c * 128:(c + 1) * 128], in_=ps(bank, 128)),
                                reads=[psb[bank], xsB[k]], writes=[xsB[k]])
                        else:
                            P.op("act", lambda e, c=c, k=k, bank=bank: e.activation(
                                out=xs[:, k, c * 128:(c + 1) * 128], in_=ps(bank, 128), func=AF.Copy),
                                reads=[psb[bank], xsB[k]], writes=[xsB[k]])
                    r0 = c0 + sub * 128
                    P.op("pool", lambda e, k=k, r0=r0: e.dma_start(out=out[r0:r0 + 128, :], in_=xs[:, k, :]),
                         reads=[xsB[k]], writes=[], dsem=xoS[k])
            for t in range(NQG):
                c_tile(t)
            P.wait_all("pool", [(s, s.count) for s in xoS])
            P.run_block()
    return nc


_CACHE = {}


def _masks(i):
    j = np.arange(128)[:, None, None]
    r = np.arange(16)[None, :, None]
    t = np.arange(512)[None, None, :]
    ok = (r * 128 + j) < (4 * i * 128 + t)
    return np.where(ok, 0.0, NEG).astype(np.float32)


def kernel(**inputs):
    x = np.asarray(inputs["x"], dtype=np.float32)
    p = np.asarray(inputs["p"], dtype=np.float32)
    B, S, _ = x.shape
    NQG = S // 2048
    if NQG not in _CACHE:
        _CACHE[NQG] = build_program(NQG)
    nc = _CACHE[NQG]
    shared = {}
    for name in ("attn_norm_g", "w_in", "sgu_norm_g", "w_s", "b_s", "q_norm_g", "k_norm_g", "w_up_a", "w_up_b",
                 "w_o", "ffn_norm_g", "w_ffn_in", "w_ffn_out", "ple_norm_g", "w_ple_gate", "w_ple"):
        shared[name] = np.ascontiguousarray(np.asarray(inputs[name], dtype=np.float32)[0])
    in_maps = []
    idxs = []
    for core in range(8):
        b, i = divmod(core, 4)
        rows = np.concatenate([np.arange((4 * m + i) * 512, (4 * m + i + 1) * 512) for m in range(NQG)])
        idxs.append((b, rows))
        d = dict(shared)
        d["xb"] = np.ascontiguousarray(x[b])
        d["xo"] = np.ascontiguousarray(x[b][rows])
        d["po"] = np.ascontiguousarray(p[0, b][rows])
        d["maskd"] = _masks(i)
        in_maps.append(d)
    res = run_bass_kernel_spmd(nc, in_maps, core_ids=list(range(8)))
    outp = np.empty((B, S, D), dtype=np.float32)
    for core in range(8):
        b, rows = idxs[core]
        outp[b, rows] = res.results[core]["out"]
    if DEBUG:
        kernel.last = res
    return outp
```
